# Optimizing a Trainium2 kernel written in Bass

```python
import math
import jax, jax.numpy as jnp
from jax import lax
import numpy as np

D_MODEL = 1024
BATCH = 4
SEQ = 8192
DEPTH = 4
DEC_BATCH = 32
DEC_SEQ = 16
PAST_LEN = 4096

CHUNK = 64
N_MIXERS = 4
N_LAYERS_A = len(range(0, DEPTH, N_MIXERS))
N_LAYERS_B = len(range(1, DEPTH, N_MIXERS))
N_LAYERS_C = len(range(2, DEPTH, N_MIXERS))
N_LAYERS_D = len(range(3, DEPTH, N_MIXERS))
GMLP_CHUNK = 128
GMLP_HALF = 3 * D_MODEL
GMLP_GROUPS = 8
GMLP_GROUP_W = GMLP_HALF // GMLP_GROUPS
DIFF_HEADS = 8
DIFF_DK = D_MODEL // (2 * DIFF_HEADS)
DIFF_DV = 2 * DIFF_DK
Q_BLOCK = 128
RET_HEADS = 4
RET_DK = D_MODEL // RET_HEADS
RET_DV = 2 * D_MODEL // RET_HEADS
HG_EXPAND = 128
HG_HEADS = D_MODEL // HG_EXPAND
HG_DK = HG_EXPAND
HG_DV = D_MODEL // HG_HEADS
FFN_HIDDEN = ((8 * D_MODEL + 3 * 256 - 1) // (3 * 256)) * 256
EPS = 1e-6
NEG_INF = -1e30
F32 = jnp.float32

kernel_name = 'hybrid_streaming_encoder_step'


def rmsnorm(x, g):
    xf = x.astype(F32)
    y = xf * lax.rsqrt(jnp.mean(xf * xf, axis=-1, keepdims=True) + EPS)
    return (y * g.astype(F32)).astype(x.dtype)


def rms_unit(x):
    xf = x.astype(F32)
    return xf * lax.rsqrt(jnp.mean(xf * xf, axis=-1, keepdims=True) + EPS)


def layernorm(x, g, b):
    xf = x.astype(F32)
    mu = jnp.mean(xf, axis=-1, keepdims=True)
    xc = xf - mu
    var = jnp.mean(xc * xc, axis=-1, keepdims=True)
    return (xc * lax.rsqrt(var + EPS) * g.astype(F32) + b.astype(F32)).astype(x.dtype)


def ada_mod(c, w, b):
    m = jax.nn.silu(c) @ w + b
    return jnp.split(m[:, None, :], 6, axis=-1)


def modulate(x, g, shift, scale):
    return rmsnorm(x, g) * (1.0 + scale) + shift


def swiglu(h, w_in, w_out):
    gate, up = jnp.split(h @ w_in, 2, axis=-1)
    return (jax.nn.silu(gate) * up) @ w_out


def gmlp_mix(h, w_in, ln_g, ln_b, w_s, b_s, w_out):
    B_, S_, _ = h.shape
    z = jax.nn.gelu(h @ w_in, approximate=False)
    u, v = jnp.split(z, 2, axis=-1)
    v = layernorm(v, ln_g, ln_b)
    T = min(S_, GMLP_CHUNK)
    vb = v.reshape(B_, S_ // T, T, GMLP_GROUPS, GMLP_GROUP_W)
    w = jnp.tril(w_s[:, :T, :T])
    mixed = jnp.einsum('gts,bnsgc->bntgc', w, vb) + b_s[:, :T].T[None, None, :, :, None]
    out = u * mixed.reshape(B_, S_, GMLP_HALF)
    return out @ w_out, v


def diff_project(h, w_in):
    B_, S_, _ = h.shape
    q, k, v = jnp.split(h @ w_in, [D_MODEL, 2 * D_MODEL], axis=-1)
    return (q.reshape(B_, S_, 2 * DIFF_HEADS, DIFF_DK),
            k.reshape(B_, S_, 2 * DIFF_HEADS, DIFF_DK),
            v.reshape(B_, S_, DIFF_HEADS, DIFF_DV))


def diff_lambda_value(lam, lam_init):
    lf = lam.astype(F32)
    return jnp.exp(jnp.sum(lf[0] * lf[1])) - jnp.exp(jnp.sum(lf[2] * lf[3])) + lam_init


def diff_weights(s, lam):
    p = jax.nn.softmax(s, axis=-1)
    B_, _, T, K = p.shape
    p = p.reshape(B_, DIFF_HEADS, 2, T, K)
    return p[:, :, 0] - lam * p[:, :, 1]


def diff_out(o, subln_g, lam_init, w_out):
    o = rmsnorm(o, subln_g) * (1.0 - lam_init)
    return o.reshape(o.shape[0], o.shape[1], DIFF_HEADS * DIFF_DV) @ w_out


def diff_attn_prompt(h, w_in, lam_p, subln_g, w_out, lam_init):
    B_, S_, _ = h.shape
    q, k, v = diff_project(h, w_in)
    lam = diff_lambda_value(lam_p, lam_init)
    scale = DIFF_DK ** -0.5
    n_blk = S_ // Q_BLOCK
    key_chunk = jnp.arange(S_) // CHUNK
    q_blocks = q.reshape(B_, n_blk, Q_BLOCK, 2 * DIFF_HEADS, DIFF_DK).swapaxes(0, 1)

    def one_block(args):
        q_blk, blk = args
        q_chunk = (blk * Q_BLOCK + jnp.arange(Q_BLOCK)) // CHUNK
        s = jnp.einsum('bthd,bshd->bhts', q_blk, k).astype(F32) * scale
        s = jnp.where(key_chunk[None, :] <= q_chunk[:, None], s, NEG_INF)
        a = diff_weights(s, lam).astype(v.dtype)
        return jnp.einsum('bhts,bshv->bthv', a, v)

    o = lax.map(one_block, (q_blocks, jnp.arange(n_blk)))
    o = o.swapaxes(0, 1).reshape(B_, S_, DIFF_HEADS, DIFF_DV)
    return diff_out(o, subln_g, lam_init, w_out), k, v


def diff_attn_sample(h, cache_k, cache_v, w_in, lam_p, subln_g, w_out, lam_init):
    B_, L, _ = h.shape
    P = cache_k.shape[1]
    q, k, v = diff_project(h, w_in)
    lam = diff_lambda_value(lam_p, lam_init)
    scale = DIFF_DK ** -0.5
    s_past = jnp.einsum('bthd,bshd->bhts', q, cache_k)
    s_new = jnp.einsum('bthd,bshd->bhts', q, k)
    s = jnp.concatenate([s_past, s_new], axis=-1).astype(F32) * scale
    q_chunk = (P + jnp.arange(L)) // CHUNK
    key_chunk = jnp.arange(P + L) // CHUNK
    s = jnp.where(key_chunk[None, :] <= q_chunk[:, None], s, NEG_INF)
    a = diff_weights(s, lam).astype(v.dtype)
    o = (jnp.einsum('bhts,bshv->bthv', a[..., :P], cache_v)
         + jnp.einsum('bhts,bshv->bthv', a[..., P:], v))
    return diff_out(o, subln_g, lam_init, w_out), k, v


def ret_log_gamma():
    return jnp.log(1.0 - 2.0 ** (-5.0 - jnp.arange(RET_HEADS, dtype=F32)))


def rotate_every_two(x):
    x1 = x[..., ::2]
    x2 = x[..., 1::2]
    return jnp.stack([-x2, x1], axis=-1).reshape(x.shape)


def xpos_rotate(x, pos):
    inv = 1.0 / (10000.0 ** jnp.linspace(0.0, 1.0, RET_DK // 2, dtype=F32))
    ang = pos.astype(F32)[:, None] * jnp.repeat(inv, 2)[None, :]
    sin = jnp.sin(ang)[None, :, None, :]
    cos = jnp.cos(ang)[None, :, None, :]
    xf = x.astype(F32)
    return xf * cos + rotate_every_two(xf) * sin


def retention_project(h, w_in, pos):
    B_, S_, _ = h.shape
    q, k, v, g = jnp.split(h @ w_in, [D_MODEL, 2 * D_MODEL, 4 * D_MODEL], axis=-1)
    q = xpos_rotate(q.reshape(B_, S_, RET_HEADS, RET_DK), pos)
    k = xpos_rotate(k.reshape(B_, S_, RET_HEADS, RET_DK), pos) * (RET_DK ** -0.5)
    v = v.reshape(B_, S_, RET_HEADS, RET_DV).astype(F32)
    return q, k, v, g


def retention_block(q, k, v, state, log_gamma):
    L = q.shape[1]
    t = jnp.arange(L, dtype=F32)
    diff = t[:, None] - t[None, :]
    decay = jnp.where(diff >= 0, jnp.exp(log_gamma[:, None, None] * jnp.maximum(diff, 0.0)), 0.0)
    scores = jnp.einsum('bthd,bshd->bhts', q, k) * decay[None]
    o = jnp.einsum('bhts,bshv->bthv', scores, v)
    cross = jnp.exp(log_gamma[None, :] * (t[:, None] + 1.0))
    o = o + jnp.einsum('bthd,bhdv->bthv', q, state) * cross[None, :, :, None]
    k_dec = k * jnp.exp(log_gamma[None, :] * (L - 1.0 - t[:, None]))[None, :, :, None]
    new_state = (jnp.exp(log_gamma * L)[None, :, None, None] * state
                 + jnp.einsum('bshd,bshv->bhdv', k_dec, v))
    return o, new_state


def retention_output(o, g, w_out):
    B_, S_ = o.shape[:2]
    o = rms_unit(o).reshape(B_, S_, 2 * D_MODEL) * jax.nn.silu(g.astype(F32))
    return o.astype(g.dtype) @ w_out


def retention_prompt(h, w_in, w_out):
    B_, S_, _ = h.shape
    q, k, v, g = retention_project(h, w_in, jnp.arange(S_))
    lg = ret_log_gamma()
    n = S_ // CHUNK

    def to_blocks(a):
        return a.reshape(B_, n, CHUNK, *a.shape[2:]).swapaxes(0, 1)

    def step(state, blk):
        o, state = retention_block(blk[0], blk[1], blk[2], state, lg)
        return state, o

    state0 = jnp.zeros((B_, RET_HEADS, RET_DK, RET_DV), F32)
    state, o = lax.scan(step, state0, (to_blocks(q), to_blocks(k), to_blocks(v)))
    o = o.swapaxes(0, 1).reshape(B_, S_, RET_HEADS, RET_DV)
    return retention_output(o, g, w_out), state.astype(h.dtype)


def retention_sample(h, state, w_in, w_out):
    B_, L, _ = h.shape
    q, k, v, g = retention_project(h, w_in, PAST_LEN + jnp.arange(L))
    o, new_state = retention_block(q, k, v, state.astype(F32), ret_log_gamma())
    return retention_output(o, g, w_out), new_state.astype(state.dtype)


def hgrn_project(h, w_in, lb):
    B_, S_, _ = h.shape
    q, f, i, g = jnp.split(h @ w_in, 4, axis=-1)
    lbf = lb.astype(F32)
    logf = jnp.logaddexp(jnp.log(lbf), jnp.log1p(-lbf) + jax.nn.log_sigmoid(f.astype(F32)))
    k = -jnp.expm1(logf)
    shp = (B_, S_, HG_HEADS, HG_DK)
    return (jax.nn.silu(q.astype(F32)).reshape(shp), k.reshape(shp),
            i.astype(F32).reshape(B_, S_, HG_HEADS, HG_DV), logf.reshape(shp), g)


def gla_block(q, k, v, logf, state):
    L = q.shape[1]
    b = jnp.cumsum(logf, axis=1)
    q_dec = q * jnp.exp(b)
    k_inv = k * jnp.exp(-b)
    causal = jnp.arange(L)[:, None] >= jnp.arange(L)[None, :]
    scores = jnp.where(causal, jnp.einsum('bthd,bshd->bhts', q_dec, k_inv), 0.0)
    o = jnp.einsum('bhts,bshv->bthv', scores, v) + jnp.einsum('bthd,bhdv->bthv', q_dec, state)
    b_last = b[:, -1]
    k_end = k * jnp.exp(b_last[:, None] - b)
    new_state = jnp.exp(b_last)[..., None] * state + jnp.einsum('bshd,bshv->bhdv', k_end, v)
    return o, new_state


def hgrn_output(o, g, norm_g, w_out):
    B_, S_ = o.shape[:2]
    o = rmsnorm(o, norm_g.reshape(HG_HEADS, HG_DV))
    o = o.reshape(B_, S_, D_MODEL) * jax.nn.silu(g.astype(F32))
    return o.astype(g.dtype) @ w_out


def hgrn_prompt(h, w_in, norm_g, w_out, lb):
    B_, S_, _ = h.shape
    q, k, v, logf, g = hgrn_project(h, w_in, lb)
    n = S_ // CHUNK

    def to_blocks(a):
        return a.reshape(B_, n, CHUNK, *a.shape[2:]).swapaxes(0, 1)

    def step(state, blk):
        o, state = gla_block(blk[0], blk[1], blk[2], blk[3], state)
        return state, o

    state0 = jnp.zeros((B_, HG_HEADS, HG_DK, HG_DV), F32)
    state, o = lax.scan(step, state0, (to_blocks(q), to_blocks(k), to_blocks(v), to_blocks(logf)))
    o = o.swapaxes(0, 1).reshape(B_, S_, HG_HEADS, HG_DV)
    return hgrn_output(o, g, norm_g, w_out), state.astype(h.dtype)


def hgrn_sample(h, state, w_in, norm_g, w_out, lb):
    q, k, v, logf, g = hgrn_project(h, w_in, lb)
    o, new_state = gla_block(q, k, v, logf, state.astype(F32))
    return hgrn_output(o, g, norm_g, w_out), new_state.astype(state.dtype)


def setup_inputs(seed: int = 0) -> dict:
    key = jax.random.key(seed)
    keys = iter(jax.random.split(key, 32))

    def nrm(shape, scale=1.0):
        return jax.random.normal(next(keys), shape, F32) * scale

    D = D_MODEL
    F = FFN_HIDDEN
    return {
        'x_prompt': nrm((BATCH, SEQ, D)),
        'x_sample': nrm((DEC_BATCH, DEC_SEQ, D)),
        'cache_k_diff': nrm((N_LAYERS_B, DEC_BATCH, PAST_LEN, 2 * DIFF_HEADS, DIFF_DK)),
        'cache_v_diff': nrm((N_LAYERS_B, DEC_BATCH, PAST_LEN, DIFF_HEADS, DIFF_DV)),
        'state_retention': nrm((N_LAYERS_C, DEC_BATCH, RET_HEADS, RET_DK, RET_DV), 0.5),
        'state_hgrn': nrm((N_LAYERS_D, DEC_BATCH, HG_HEADS, HG_DK, HG_DV), 0.3),
        'c_prompt': nrm((BATCH, D)),
        'c_sample': nrm((DEC_BATCH, D)),
        'w_ada': nrm((DEPTH, D, 6 * D), 0.5 * D ** -0.5),
        'b_ada': nrm((DEPTH, 6 * D), 0.01),
        'norm_gains': 1.0 + nrm((DEPTH, 4, D), 0.05),
        'gmlp_w_in': nrm((N_LAYERS_A, D, 2 * GMLP_HALF), D ** -0.5),
        'gmlp_ln_g': 1.0 + nrm((N_LAYERS_A, GMLP_HALF), 0.05),
        'gmlp_ln_b': nrm((N_LAYERS_A, GMLP_HALF), 0.02),
        'gmlp_w_s': nrm((N_LAYERS_A, GMLP_GROUPS, GMLP_CHUNK, GMLP_CHUNK), GMLP_CHUNK ** -0.5),
        'gmlp_b_s': 1.0 + nrm((N_LAYERS_A, GMLP_GROUPS, GMLP_CHUNK), 0.05),
        'gmlp_w_out': nrm((N_LAYERS_A, GMLP_HALF, D), GMLP_HALF ** -0.5),
        'diff_w_in': nrm((N_LAYERS_B, D, 3 * D), D ** -0.5),
        'diff_lambda': nrm((N_LAYERS_B, 4, DIFF_DK), 0.1),
        'diff_subln': 1.0 + nrm((N_LAYERS_B, DIFF_DV), 0.05),
        'diff_w_out': nrm((N_LAYERS_B, D, D), D ** -0.5),
        'ret_w_in': nrm((N_LAYERS_C, D, 6 * D), D ** -0.5),
        'ret_w_out': nrm((N_LAYERS_C, 2 * D, D), (2 * D) ** -0.5),
        'hgrn_w_in': nrm((N_LAYERS_D, D, 4 * D), D ** -0.5),
        'hgrn_norm': 1.0 + nrm((N_LAYERS_D, D), 0.05),
        'hgrn_w_out': nrm((N_LAYERS_D, D, D), D ** -0.5),
        'hgrn_lower_bounds': nrm((DEPTH, D), 0.1),
        'ffn_w_in': nrm((DEPTH, D, 2 * F), D ** -0.5),
        'ffn_w_out': nrm((DEPTH, F, D), F ** -0.5),
    }


def reference(x_prompt, x_sample, cache_k_diff, cache_v_diff, state_retention, state_hgrn,
              c_prompt, c_sample, w_ada, b_ada, norm_gains,
              gmlp_w_in, gmlp_ln_g, gmlp_ln_b, gmlp_w_s, gmlp_b_s, gmlp_w_out,
              diff_w_in, diff_lambda, diff_subln, diff_w_out,
              ret_w_in, ret_w_out,
              hgrn_w_in, hgrn_norm, hgrn_w_out, hgrn_lower_bounds,
              ffn_w_in, ffn_w_out):
    lb_p = jax.nn.softmax(hgrn_lower_bounds.astype(F32), axis=0)
    lower = jnp.cumsum(lb_p, axis=0) - lb_p[0]

    yp, ys = x_prompt, x_sample
    gmlp_v_s = []
    kd_p, vd_p, kd_s, vd_s = [], [], [], []
    ret_p, ret_s, hg_p, hg_s = [], [], [], []
    for i in range(DEPTH):
        kind, j = i % N_MIXERS, i // N_MIXERS
        mp = ada_mod(c_prompt, w_ada[i], b_ada[i])
        ms = ada_mod(c_sample, w_ada[i], b_ada[i])
        hp = modulate(yp, norm_gains[i, 0], mp[0], mp[1])
        hs = modulate(ys, norm_gains[i, 0], ms[0], ms[1])
        if kind == 0:
            op, _ = gmlp_mix(hp, gmlp_w_in[j], gmlp_ln_g[j], gmlp_ln_b[j], gmlp_w_s[j], gmlp_b_s[j], gmlp_w_out[j])
            os_, v_rows = gmlp_mix(hs, gmlp_w_in[j], gmlp_ln_g[j], gmlp_ln_b[j], gmlp_w_s[j], gmlp_b_s[j], gmlp_w_out[j])
            gmlp_v_s.append(v_rows)
        elif kind == 1:
            lam_init = 0.8 - 0.6 * math.exp(-0.3 * i)
            op, k_new, v_new = diff_attn_prompt(hp, diff_w_in[j], diff_lambda[j], diff_subln[j], diff_w_out[j], lam_init)
            kd_p.append(k_new)
            vd_p.append(v_new)
            os_, k_new, v_new = diff_attn_sample(hs, cache_k_diff[j], cache_v_diff[j], diff_w_in[j], diff_lambda[j],
                                                 diff_subln[j], diff_w_out[j], lam_init)
            kd_s.append(k_new)
            vd_s.append(v_new)
        elif kind == 2:
            op, st = retention_prompt(hp, ret_w_in[j], ret_w_out[j])
            ret_p.append(st)
            os_, st = retention_sample(hs, state_retention[j], ret_w_in[j], ret_w_out[j])
            ret_s.append(st)
        else:
            op, st = hgrn_prompt(hp, hgrn_w_in[j], hgrn_norm[j], hgrn_w_out[j], lower[i])
            hg_p.append(st)
            os_, st = hgrn_sample(hs, state_hgrn[j], hgrn_w_in[j], hgrn_norm[j], hgrn_w_out[j], lower[i])
            hg_s.append(st)
        yp = yp + mp[2] * rmsnorm(op, norm_gains[i, 1])
        ys = ys + ms[2] * rmsnorm(os_, norm_gains[i, 1])
        hp = modulate(yp, norm_gains[i, 2], mp[3], mp[4])
        hs = modulate(ys, norm_gains[i, 2], ms[3], ms[4])
        yp = yp + mp[5] * rmsnorm(swiglu(hp, ffn_w_in[i], ffn_w_out[i]), norm_gains[i, 3])
        ys = ys + ms[5] * rmsnorm(swiglu(hs, ffn_w_in[i], ffn_w_out[i]), norm_gains[i, 3])

    return (yp, ys, jnp.stack(gmlp_v_s), jnp.stack(kd_p), jnp.stack(vd_p), jnp.stack(kd_s), jnp.stack(vd_s),
            jnp.stack(ret_p), jnp.stack(ret_s), jnp.stack(hg_p), jnp.stack(hg_s))
```

```python
import math
import numpy as np
import concourse.bass as bass
import concourse.mybir as mybir
from concourse.bass_utils import run_bass_kernel_spmd
from contextlib import ExitStack

F32 = mybir.dt.float32
BF16 = mybir.dt.bfloat16
AF = mybir.ActivationFunctionType
ALU = mybir.AluOpType

D = 1024
FH = 2816
GH = 3072
EPS = 1e-6
ENGS = ['pe', 'act', 'dve', 'pool', 'sp']
EPOCH = 20000
NSLOT = 3
SLOT_EL = 4096
LAM_INIT = 0.8 - 0.6 * math.exp(-0.3 * 1)
RET_G = [1.0 - 2.0 ** (-5.0 - h) for h in range(4)]


class Sched:
    def __init__(s):
        s.ops = {e: [] for e in ENGS}
        s.lastw = {}
        s.readers = {}
        s.chan_n = {}

    def _deps(s, eng, reads, writes):
        d = {}

        def need(tok):
            if tok[0] == 'e':
                _, e, i = tok
                if e == eng and e == 'pe':
                    return
                d[('e', e)] = max(d.get(('e', e), -1), i)
            else:
                _, c, v = tok
                d[('d', c)] = max(d.get(('d', c), -1), v)
        for k in reads:
            if k in s.lastw:
                need(s.lastw[k])
        for k in writes:
            if k in s.lastw:
                need(s.lastw[k])
            for tok in s.readers.get(k, {}).values():
                need(tok)
        return d

    def op(s, eng, fn, reads=(), writes=()):
        writes = list(writes) + [k for k in reads if isinstance(k, tuple) and k[0] == 'ps' and k not in writes]
        idx = len(s.ops[eng])
        d = s._deps(eng, reads, writes)
        s.ops[eng].append(dict(fn=fn, deps=d, sig=False, kind='op'))
        tok = ('e', eng, idx)
        for k in reads:
            s.readers.setdefault(k, {})[('e', eng)] = tok
        for k in writes:
            s.lastw[k] = tok
            s.readers[k] = {}
        return tok

    def dma(s, q, chan, fns, reads=(), writes=()):
        d = s._deps(q, reads, writes)
        if s.chan_n.get(chan, 0) > 0:
            d[('d', chan)] = max(d.get(('d', chan), -1), s.chan_n[chan])
        s.chan_n[chan] = s.chan_n.get(chan, 0) + len(fns)
        v = s.chan_n[chan]
        s.ops[q].append(dict(fn=fns, deps=d, kind='dma', chan=chan))
        tok = ('d', chan, v)
        for k in reads:
            s.readers.setdefault(k, {})[('d', chan)] = tok
        for k in writes:
            s.lastw[k] = tok
            s.readers[k] = {}
        return tok

    def fence(s):
        last = {e: len(s.ops[e]) - 1 for e in ENGS}
        chans = dict(s.chan_n)
        for e in ENGS:
            d = {}
            for e2 in ENGS:
                if e2 != e and last[e2] >= 0:
                    d[('e', e2)] = last[e2]
            for c, v in chans.items():
                d[('d', c)] = v
            s.ops[e].append(dict(fn=None, deps=d, sig=False, kind='nop'))

    def finish(s):
        d = {('d', c): v for c, v in s.chan_n.items()}
        for e2 in ENGS:
            if e2 != 'sp' and len(s.ops[e2]) > 0:
                d[('e', e2)] = len(s.ops[e2]) - 1
        s.ops['sp'].append(dict(fn=None, deps=d, sig=False, kind='nop'))

    def emit(s, nc, es):
        for e in ENGS:
            for o in s.ops[e]:
                for key, val in o['deps'].items():
                    if key[0] == 'e':
                        tgt = s.ops[key[1]][val]
                        j = val
                        while s.ops[key[1]][j]['kind'] != 'op':
                            j -= 1
                            if j < 0:
                                break
                        if j >= 0:
                            s.ops[key[1]][j]['sig'] = True
        nsig = {}
        for e in ENGS:
            c = 0
            for o in s.ops[e]:
                if o.get('sig'):
                    c += 1
                    o['signum'] = c
                o['cum'] = c
            nsig[e] = c
        sems = {e: [es.enter_context(nc.semaphore("s_%s_%d" % (e, i))) for i in range(max(1, (nsig[e] + EPOCH - 1) // EPOCH))]
                for e in ENGS}
        dsems = {c: es.enter_context(nc.semaphore("d_%d" % i)) for i, c in enumerate(s.chan_n.keys())}
        block = es.enter_context(nc.Block())

        def run(ename, h):
            waited = {}
            for o in s.ops[ename]:
                for key, val in o['deps'].items():
                    if waited.get(key, -1) >= val:
                        continue
                    waited[key] = val
                    if key[0] == 'e':
                        j = val
                        while j >= 0 and s.ops[key[1]][j]['kind'] != 'op':
                            j -= 1
                        if j < 0:
                            continue
                        sn = s.ops[key[1]][j]['signum']
                        h.wait_ge(sems[key[1]][(sn - 1) // EPOCH], (sn - 1) % EPOCH + 1)
                    else:
                        h.wait_ge(dsems[key[1]], 16 * val)
                if o['kind'] == 'op':
                    ins = o['fn'](h)
                    if o['sig']:
                        sn = o['signum']
                        ins.then_inc(sems[ename][(sn - 1) // EPOCH], 1)
                elif o['kind'] == 'dma':
                    for f in o['fn']:
                        f(h).then_inc(dsems[o['chan']], 16)

        @block.tensor
        def _(h):
            run('pe', h)

        @block.scalar
        def _(h):
            run('act', h)

        @block.vector
        def _(h):
            run('dve', h)

        @block.gpsimd
        def _(h):
            run('pool', h)

        @block.sync
        def _(h):
            run('sp', h)


def make_consts(S, P):
    c = {}
    c['ident'] = np.eye(128, dtype=np.float32)
    t = np.arange(128)
    c['tril'] = (t[:, None] >= t[None, :]).astype(np.float32)
    R = np.zeros((128, 128), np.float32)
    for i in range(64):
        R[2 * i + 1, 2 * i] = -1.0
        R[2 * i, 2 * i + 1] = 1.0
    c['rotR'] = R
    inv = (1.0 / (10000.0 ** np.linspace(0.0, 1.0, 128, dtype=np.float32))).astype(np.float32)
    invf = np.repeat(inv, 2)

    def sincos(pos):
        ang = pos.astype(np.float32)[None, :] * invf[:, None]
        sn = np.sin(ang).astype(np.float32)
        cs = np.cos(ang).astype(np.float32)
        return (cs.reshape(2, 128, -1).transpose(1, 0, 2).copy(),
                sn.reshape(2, 128, -1).transpose(1, 0, 2).copy())
    c['cos_p'], c['sin_p'] = sincos(np.arange(S))
    slot = np.arange(128)
    c['cos_s'], c['sin_s'] = sincos(P + (slot % 32))
    lg = np.log(np.array(RET_G, np.float64))
    tl_p = (np.arange(512) % 128).astype(np.float64)
    tl_s = (slot % 32).astype(np.float64)
    c['cross_p'] = np.broadcast_to(np.exp(lg[:, None] * (tl_p[None, :] + 1.0))[None], (128, 4, 512)).astype(np.float32).copy()
    c['kinv_p'] = np.broadcast_to(np.exp(-lg[:, None] * (tl_p[None, :] + 1.0))[None], (128, 4, 512)).astype(np.float32).copy()
    c['cross_s'] = np.broadcast_to(np.exp(lg[:, None] * (tl_s[None, :] + 1.0))[None], (128, 4, 128)).astype(np.float32).copy()
    c['kinv_s'] = np.broadcast_to(np.exp(-lg[:, None] * (tl_s[None, :] + 1.0))[None], (128, 4, 128)).astype(np.float32).copy()
    real = (slot % 32) < 16
    strm = slot // 32
    c['mask_ret_p'] = (t[:, None] <= t[None, :]).astype(np.float32)
    m = (strm[:, None] == strm[None, :]) & real[:, None] & real[None, :] & (slot[:, None] <= slot[None, :])
    c['mask_blk_s'] = m.astype(np.float32)
    c['mask_hg_p'] = ((t[:, None] // 64 == t[None, :] // 64) & (t[:, None] <= t[None, :])).astype(np.float32)
    rm = np.zeros((128, 4), np.float32)
    for i in range(4):
        rm[(strm == i) & real, i] = 1.0
    c['rowmask_s'] = rm
    rp = np.zeros((128, 2), np.float32)
    rp[:64, 0] = 1.0
    rp[64:, 1] = 1.0
    c['rowmask_hp'] = rp
    return c


CONST_SHAPES = None


class Builder:
    def __init__(s, S, P):
        s.S_len = S
        s.P = P
        s.nc = bass.Bass("TRN2", target_bir_lowering=False)
        s.sch = Sched()
        s.es = ExitStack()
        s.rot_i = 0
        s.slot_i = 0
        s.tmp_i = 0
        s.out_ch = 0
        s.dr = {}

    def din(s, name, shape, dt=F32):
        s.dr[name] = s.nc.dram_tensor(name, list(shape), dt, kind="ExternalInput").ap()
        return s.dr[name]

    def dout(s, name, shape):
        s.dr[name] = s.nc.dram_tensor(name, list(shape), F32, kind="ExternalOutput").ap()
        return s.dr[name]

    def dint(s, name, shape, dt=BF16):
        s.dr[name] = s.nc.dram_tensor(name, list(shape), dt, kind="Internal").ap()
        return s.dr[name]

    def sb(s, name, shape, dt=F32):
        return s.es.enter_context(s.nc.sbuf_tensor('sb_' + name, list(shape), dt))

    def mm(s, out, lhsT, rhs, start, stop, r, w, skip=False):
        s.sch.op('pe', lambda h: h.matmul(out, lhsT, rhs, start=start, stop=stop, skip_group_check=skip), r, w)

    def tr(s, out, in_, ident, r, w):
        s.sch.op('pe', lambda h: h.transpose(out, in_, ident), r, w)

    def act(s, out, in_, func, r, w, bias=None, scale=None):
        kw = {}
        if bias is not None:
            kw['bias'] = bias
        if scale is not None:
            kw['scale'] = scale
        s.sch.op('act', lambda h: h.activation(out, in_, func, **kw), r, w)

    def tt(s, out, a, b, op, r, w, eng='dve'):
        s.sch.op(eng, lambda h: h.tensor_tensor(out, a, b, op), r, w)

    def ts(s, out, a, s1, s2, op0, op1, r, w, eng='dve'):
        if op1 is None:
            s.sch.op(eng, lambda h: h.tensor_scalar(out, a, s1, None, op0), r, w)
        else:
            s.sch.op(eng, lambda h: h.tensor_scalar(out, a, s1, s2, op0, op1), r, w)

    def stt(s, out, a, sc, b, op0, op1, r, w):
        s.sch.op('dve', lambda h: h.scalar_tensor_tensor(out, a, sc, b, op0, op1), r, w)

    def cp(s, out, in_, r, w, eng='dve'):
        s.sch.op(eng, lambda h: h.tensor_copy(out, in_), r, w)

    def recip(s, out, in_, r, w):
        s.sch.op('dve', lambda h: h.reciprocal(out, in_), r, w)

    def memset(s, ap, val, w, eng='dve'):
        s.sch.op(eng, lambda h: h.memset(ap, val), (), w)

    def load(s, out, in_, chan, r, w, q='sp', **kw):
        s.sch.dma(q, chan, [lambda h: h.dma_start(out=out, in_=in_, **kw)], r, w)

    def store(s, out, in_, r, w):
        ch = ('st', s.out_ch % 8)
        s.out_ch += 1
        s.sch.dma('pool', ch, [lambda h: h.dma_start(out=out, in_=in_)], r, w)

    def rot(s):
        i = s.rot_i % 4
        s.rot_i += 1
        return i

    def convert_weight(s, name, src2d, R, N, KC, BW, ffn_pair=False, layers=None):
        L = R // (KC * 128)
        BWt = 2 * BW if ffn_pair else BW
        NBLK = (N // 2 // BW) if ffn_pair else (N // BW)
        if name not in s.wb:
            s.wb[name] = s.dint('wb_' + name, [L, NBLK, 128, KC * BWt])
            s.wmeta[name] = (KC, BW, ffn_pair)
        dst = s.wb[name]
        for li in (range(L) if layers is None else layers):
            for kc in range(KC):
                rb = li * KC + kc
                ch = ('cv', s.cv_i % 8)
                s.cv_i += 1
                rows = src2d[rb * 128:(rb + 1) * 128, :]
                dv = dst[li].rearrange("b p (k n) -> p b k n", k=KC)[:, :, kc, :]
                if ffn_pair:
                    fns = []
                    for half in range(2):
                        sv = rows[:, half * (N // 2):(half + 1) * (N // 2)].rearrange("p (b n) -> p b n", n=BW)
                        fns.append(lambda h, sv=sv, dv=dv, half=half: h.dma_start(out=dv[:, :, half * BW:(half + 1) * BW], in_=sv, max_dma_last_dim=4096))
                else:
                    sv = rows.rearrange("p (b n) -> p b n", n=BW)
                    fns = [lambda h, sv=sv, dv=dv: h.dma_start(out=dv, in_=sv, max_dma_last_dim=4096)]
                s.sch.dma('pool', ch, fns, (), [('wbf', name, rb)])

    def wload(s, name, r0, KCn, c0, ncols):
        KC, BW, pair = s.wmeta[name]
        assert KCn == KC and (pair or ncols == BW) and c0 % BW == 0, (name, KCn, c0, ncols)
        BWt = 2 * BW if pair else BW
        i = s.slot_i % NSLOT
        s.slot_i += 1
        assert KC * BWt <= SLOT_EL
        li = r0 // (KC * 128)
        flat = s.wslot[i][:, 0:KC * BWt]
        view = flat.rearrange("p (k n) -> p k n", k=KC)
        key = ('wslot', i)
        s.load(flat, s.wb[name][li, c0 // BW], ('w', i), [('wbf', name, li * KC + k) for k in range(KC)], [key])
        return view, key

    def linear_fm(s, wname, row0, KCn, col0, ncols, rhs_fn, T, evac, blk=512):
        done = 0
        while done < ncols:
            nb = min(blk, ncols - done)
            wv, wkey = s.wload(wname, row0, KCn, col0 + done, nb)
            for m_ in range(nb // 128):
                pi = s.rot()
                ps = s.ps[pi]
                for kc in range(KCn):
                    rap, rkey = rhs_fn(kc)
                    s.mm(ps[:, :T], wv[:, kc, m_ * 128:(m_ + 1) * 128], rap, kc == 0, kc == KCn - 1,
                         [wkey, rkey], [('ps', pi)])
                evac(done // 128 + m_, ps[:, :T], ('ps', pi))
            done += nb

    def linear_tm(s, wname, row0, KCn, col0, ncols, lhs_fn, NB, evac):
        for b in range(ncols // 512):
            wv, wkey = s.wload(wname, row0, KCn, col0 + b * 512, 512)
            for tb in range(NB):
                pi = s.rot()
                ps = s.ps[pi]
                for kc in range(KCn):
                    lap, lkey = lhs_fn(kc, tb)
                    s.mm(ps[:, :512], lap, wv[:, kc, :], kc == 0, kc == KCn - 1, [wkey, lkey], [('ps', pi)])
                evac(tb, b, ps[:, :512], ('ps', pi))

    def rstd_from_sq(s, sq_fn, C, nfeat, T, out_ap, out_key):
        pi = s.rot()
        ps = s.ps[pi]
        for c in range(C):
            ap_, k_ = sq_fn(c)
            s.mm(ps[:, :T], s.ones_bf[:, :], ap_, c == 0, c == C - 1, ['ones', k_], [('ps', pi)])
        s.act(s.rs_tmp[:, :T], ps[:, :T], AF.Sqrt, [('ps', pi), 'eps'], ['tmpf'], bias=s.eps_t[:, 0:1], scale=1.0 / nfeat)
        s.recip(out_ap, s.rs_tmp[:, :T], ['tmpf'], [out_key])

    def modulate(s, l, which, T, streams):
        xT, hT, sqT = s.xT, s.hT, s.sqT
        s.act(sqT[:, :, :T], xT[:, :, :T], AF.Square, ['xT'], ['sqT'])
        s.rstd_from_sq(lambda c: (sqT[:, c, :T], 'sqT'), 8, D, T, s.rstd[:, :T], 'rstd')
        for c in range(8):
            tmp, tk = (s.tmpf, 'tmpf') if c % 2 == 0 else (s.stage[1], ('stage', 1))
            s.tt(tmp[:, :T], xT[:, c, :T], s.rstd[:, :T], ALU.mult, ['xT', 'rstd'], [tk])
            for (c0, c1, si) in streams:
                s.act(hT[:, c, c0:c1], tmp[:, c0:c1], AF.Identity, [tk, 'comb'], [('hT', c)],
                      bias=s.comb[:, l, 3 * which + 0, c, si:si + 1], scale=s.comb[:, l, 3 * which + 1, c, si:si + 1])

    def evac_out(s, m, ps, pkey, T):
        s.cp(s.outT[:, m, :T], ps, [pkey], [('outT', m)])
        s.act(s.sqT[:, m, :T], ps, AF.Square, [pkey], ['sqT'])

    def postnorm(s, l, which, T, streams):
        s.rstd_from_sq(lambda c: (s.sqT[:, c, :T], 'sqT'), 8, D, T, s.rstd[:, :T], 'rstd')
        for c in range(8):
            s.tt(s.tmpf[:, :T], s.outT[:, c, :T], s.rstd[:, :T], ALU.mult, [('outT', c), 'rstd'], ['tmpf'])
            for (c0, c1, si) in streams:
                s.stt(s.xT[:, c, c0:c1], s.tmpf[:, c0:c1], s.comb[:, l, 3 * which + 2, c, si:si + 1], s.xT[:, c, c0:c1],
                      ALU.mult, ALU.add, ['tmpf', 'comb', 'xT'], ['xT'])

    def hkeys(s):
        return [('hT', c) for c in range(8)]

    def ffn(s, l, T, streams):
        s.modulate(l, 1, T, streams)
        aT = s.arena_bf[:, 0:22 * 512].rearrange("p (c t) -> p c t", c=22)
        sg = s.arena_bf[:, 22 * 512:24 * 512].rearrange("p (c t) -> p c t", c=2)
        for jb in range(11):
            wgu, kg = s.wload('ffn_in', l * D, 8, jb * 256, 256)
            ku = kg
            wg = wgu[:, :, 0:256]
            wu = wgu[:, :, 256:512]
            for jj in range(2):
                j = 2 * jb + jj
                pg = s.rot()
                for kc in range(8):
                    s.mm(s.ps[pg][:, :T], wg[:, kc, jj * 128:(jj + 1) * 128], s.hT[:, kc, :T], kc == 0, kc == 7,
                         [kg, ('hT', kc)], [('ps', pg)])
                pu = s.rot()
                for kc in range(8):
                    s.mm(s.ps[pu][:, :T], wu[:, kc, jj * 128:(jj + 1) * 128], s.hT[:, kc, :T], kc == 0, kc == 7,
                         [ku, ('hT', kc)], [('ps', pu)])
                s.act(sg[:, j % 2, :T], s.ps[pg][:, :T], AF.Silu, [('ps', pg)], [('sg', j % 2)])
                s.tt(aT[:, j, :T], sg[:, j % 2, :T], s.ps[pu][:, :T], ALU.mult, [('sg', j % 2), ('ps', pu)], [('aT', j)])
        s.linear_fm('ffn_out', l * FH, 22, 0, D, lambda kc: (aT[:, kc, :T], ('aT', kc)), T,
                    lambda m, ps, pk: s.evac_out(m, ps, pk, T), blk=128)
        s.postnorm(l, 1, T, streams)
        s.sch.fence()

    def gmlp(s, T, streams, sample):
        NB = T // 128
        s.modulate(0, 0, T, streams)
        uT = s.arena_bf[:, 0:24 * T].rearrange("p (c t) -> p c t", c=24)
        vtm = s.arena_bf[:, 24 * T:24 * T + NB * GH].rearrange("p (b f) -> p b f", b=NB)
        s.linear_fm('gmlp_in', 0, 8, 0, GH, lambda kc: (s.hT[:, kc, :T], ('hT', kc)), T,
                    lambda m, ps, pk: s.act(uT[:, m, :T], ps, AF.Gelu, [pk], [('uT', m)]))
        s.linear_tm('gmlp_in', 0, 8, GH, GH, lambda kc, tb: (s.hT[:, kc, tb * 128:(tb + 1) * 128], ('hT', kc)), NB,
                    lambda tb, b, ps, pk: s.act(vtm[:, tb, b * 512:(b + 1) * 512], ps, AF.Gelu, [pk], [('vtm', tb)]))
        for tb in range(NB):
            for b in range(6):
                s.sch.op('dve', lambda h, tb=tb, b=b: h.bn_stats(s.bnst[:, b, :], vtm[:, tb, b * 512:(b + 1) * 512]),
                         [('vtm', tb)], ['bnst'])
            s.sch.op('dve', lambda h: h.bn_aggr(s.bnag[:, :], s.bnst[:, :, :].rearrange("p a b -> p (a b)")), ['bnst'], ['bnag'])
            s.act(s.col_a[:, 0:1], s.bnag[:, 1:2], AF.Sqrt, ['bnag', 'eps'], ['col_a'], bias=s.eps_t[:, 0:1], scale=1.0)
            s.recip(s.col_b[:, 0:1], s.col_a[:, 0:1], ['col_a'], ['col_b'])
            s.stt(s.col_c[:, 0:1], s.bnag[:, 0:1], -1.0, s.col_b[:, 0:1], ALU.mult, ALU.mult, ['bnag', 'col_b'], ['col_c'])
            s.act(vtm[:, tb, :], vtm[:, tb, :], AF.Identity, [('vtm', tb), 'col_b', 'col_c'], [('vtm', tb)],
                  bias=s.col_c[:, 0:1], scale=s.col_b[:, 0:1])
            if sample:
                s.tt(s.big_f[:, 0:GH], vtm[:, tb, :], s.lng_b[:, :], ALU.mult, [('vtm', tb), 'lngb'], ['big_f'])
                s.tt(s.big_f[:, 0:GH], s.big_f[:, 0:GH], s.lnb_b[:, :], ALU.add, ['big_f', 'lngb'], ['big_f'])
                for i in range(4):
                    s.store(s.dr['o_gv'][i * 16:(i + 1) * 16, :], s.big_f[i * 32:i * 32 + 16, 0:GH], ['big_f'], [])
        spw = s.spw_s if sample else s.spw_p
        rows2 = s.rows2_s if sample else s.rows2_p
        for c in range(24):
            g = c // 3
            pi = s.rot()
            ps = s.ps[pi]
            for tb in range(NB):
                s.mm(ps[:, tb * 128:(tb + 1) * 128], s.lrow[0:2, c * 128:(c + 1) * 128], rows2[0:2, g, :], tb == 0, False, ['lrow', 'rows2'], [('ps', pi)], skip=True)
                s.mm(ps[:, tb * 128:(tb + 1) * 128], vtm[:, tb, c * 128:(c + 1) * 128], spw[:, g, :], False, True,
                     [('vtm', tb), 'spw'], [('ps', pi)], skip=True)
            s.stt(uT[:, c, :T], ps[:, :T], s.lng_fm[:, c:c + 1], uT[:, c, :T], ALU.mult, ALU.mult, [('ps', pi), 'lng_fm', ('uT', c)],
                  [('uT', c)])
        s.linear_fm('gmlp_out', 0, 24, 0, D, lambda kc: (uT[:, kc, :T], ('uT', kc)), T,
                    lambda m, ps, pk: s.evac_out(m, ps, pk, T), blk=128)
        s.postnorm(0, 0, T, streams)
        s.sch.fence()

    def attn_common_proj(s, T, NB, qT, kT, vbf, k_out, v_out, row_of):
        s.linear_fm('diff_in', 0, 8, 0, D, lambda kc: (s.hT[:, kc, :T], ('hT', kc)), T,
                    lambda m, ps, pk: s.act(qT[:, m, :T], ps, AF.Copy, [pk], [('qT', m)], scale=0.125))
        s.linear_fm('diff_in', 0, 8, D, D, lambda kc: (s.hT[:, kc, :T], ('hT', kc)), T,
                    lambda m, ps, pk: s.cp(kT[:, m, :T], ps, [pk], [('kT', m)]))

        def ev_k(tb, b, ps, pk):
            st = s.stage[s.tmp_i % 2]
            sk = ('stage', s.tmp_i % 2)
            s.tmp_i += 1
            s.cp(st[:, 0:512], ps, [pk], [sk], eng='dve')
            for (r0, n, d0) in row_of(tb):
                s.store(k_out[d0:d0 + n, b * 512:(b + 1) * 512], st[r0:r0 + n, 0:512], [sk], [])
        s.linear_tm('diff_in', 0, 8, D, D, lambda kc, tb: (s.hT[:, kc, tb * 128:(tb + 1) * 128], ('hT', kc)), NB, ev_k)

        def ev_v(tb, b, ps, pk):
            st = s.stage[s.tmp_i % 2]
            sk = ('stage', s.tmp_i % 2)
            s.tmp_i += 1
            s.cp(st[:, 0:512], ps, [pk], [sk], eng='dve')
            s.act(vbf[:, tb, b * 512:(b + 1) * 512], ps, AF.Copy, [pk], [('vbf', tb)])
            for (r0, n, d0) in row_of(tb):
                s.store(v_out[d0:d0 + n, b * 512:(b + 1) * 512], st[r0:r0 + n, 0:512], [sk], [])
        s.linear_tm('diff_in', 0, 8, 2 * D, D, lambda kc, tb: (s.hT[:, kc, tb * 128:(tb + 1) * 128], ('hT', kc)), NB, ev_v)

    def attn_finalize(s, o1, o2, z1, z2, keys, W, out_ap, out_key):
        f = s.fin
        s.recip(f[:, 0, :W], z1, [keys[2]], [('fin', 0)])
        s.recip(f[:, 1, :W], z2, [keys[3]], [('fin', 1)])
        s.tt(f[:, 0, :W], o1, f[:, 0, :W], ALU.mult, [keys[0], ('fin', 0)], [('fin', 0)])
        s.tt(f[:, 1, :W], o2, f[:, 1, :W], ALU.mult, [keys[1], ('fin', 1)], [('fin', 1)])
        s.stt(f[:, 2, :W], f[:, 1, :W], s.neglam[:, 0:1], f[:, 0, :W], ALU.mult, ALU.add, [('fin', 0), ('fin', 1), 'neglam'], [('fin', 2)])
        s.act(s.fin_sq[:, :W], f[:, 2, :W], AF.Square, [('fin', 2)], ['fin_sq'])
        s.rstd_from_sq(lambda c: (s.fin_sq[:, :W], 'fin_sq'), 1, 128, W, f[:, 3, :W], ('fin', 3))
        s.tt(f[:, 2, :W], f[:, 2, :W], f[:, 3, :W], ALU.mult, [('fin', 2), ('fin', 3)], [('fin', 2)])
        s.act(out_ap, f[:, 2, :W], AF.Identity, [('fin', 2), 'subg'], [out_key], scale=s.subg[:, 0:1])

    def attn_prompt(s, st_i):
        T = 512
        t0 = st_i * 512
        streams = [(0, 512, 0)]
        s.modulate(1, 0, T, streams)
        A = s.arena_bf
        qT = A[:, 0:4096].rearrange("p (c t) -> p c t", c=8)
        kT = A[:, 4096:8192].rearrange("p (c t) -> p c t", c=8)
        vbf = A[:, 8192:12288].rearrange("p (b f) -> p b f", b=4)
        atT = A[:, 12288:16384].rearrange("p (c t) -> p c t", c=8)
        E = A[:, 16384:16384 + 2048].rearrange("p (a t) -> p a t", a=4)
        kspan = [A[:, 18432 + i * 2048:18432 + (i + 1) * 2048] for i in range(2)]
        vspan = [A[:, 22528 + i * 2048:22528 + (i + 1) * 2048].rearrange("p (b d) -> p b d", b=16) for i in range(2)]
        s.attn_common_proj(T, 4, qT, kT, vbf, s.dr['o_kp'], s.dr['o_vp'], lambda tb: [(0, 128, t0 + tb * 128)])
        if st_i < s.S_len // 512 - 1:
            s.store(s.dr['kscr'][:, :, t0:t0 + 512].rearrange("h p t -> p h t"), kT[:, :, :], [('kT', m) for m in range(8)], [('kscr', st_i)])
            for hh in range(8):
                s.store(s.dr['vscr'][hh, :, st_i * 4:(st_i + 1) * 4, :], vbf[:, :, hh * 128:(hh + 1) * 128], [('vbf', b) for b in range(4)],
                        [('vscr', st_i, hh)])
        npre = st_i * 4
        SPB = s.SPB
        for hh in range(8):
            spans = list(range(0, npre, SPB))
            nblk_total = npre + 4

            def load_span(sp0):
                n = min(SPB, npre - sp0)
                bi = s.span_i % 2
                s.span_i += 1
                s.load(kspan[bi][:, 0:n * 128], s.dr['kscr'][hh, :, sp0 * 128:(sp0 + n) * 128], ('ks', bi),
                       [('kscr', j) for j in range(sp0 // 4, (sp0 + n + 3) // 4)], [('kspan', bi)])
                s.load(vspan[bi][:, 0:n, :], s.dr['vscr'][hh, :, sp0:sp0 + n, :], ('vs', bi),
                       [('vscr', j, hh) for j in range(sp0 // 4, (sp0 + n + 3) // 4)], [('vspan', bi)])
                return [(kspan[bi][:, j * 128:(j + 1) * 128], ('kspan', bi), vspan[bi][:, j, :], ('vspan', bi), 0, False) for j in range(n)]

            def qk_exp(bi_, blk):
                (kap, kkey, vap, vkey, c0, diag) = blk
                pa = (bi_ % 2) * 2
                W = 512 - c0
                s.mm(s.ps[pa][:, :W], kap[0:64, :], qT[0:64, hh, c0:512], True, True, [kkey, ('qT', hh)], [('ps', pa)])
                s.mm(s.ps[pa + 1][:, :W], kap[64:128, :], qT[64:128, hh, c0:512], True, True, [kkey, ('qT', hh)], [('ps', pa + 1)])
                s.act(E[:, pa, :W], s.ps[pa][:, :W], AF.Exp, [('ps', pa)], [('E', pa)])
                s.act(E[:, pa + 1, :W], s.ps[pa + 1][:, :W], AF.Exp, [('ps', pa + 1)], [('E', pa + 1)])
                if diag:
                    s.memset(E[64:128, pa, 0:64], 0.0, [('E', pa)], eng='pool')
                    s.memset(E[64:128, pa + 1, 0:64], 0.0, [('E', pa + 1)], eng='pool')

            def pv(bi_, blk):
                (kap, kkey, vap, vkey, c0, diag) = blk
                pa = (bi_ % 2) * 2
                W = 512 - c0
                first = (bi_ == 0)
                last = (bi_ == nblk_total - 1)
                s.mm(s.ps[4][:, c0:512], vap, E[:, pa, :W], first, last, [vkey, ('E', pa)], [('ps', 4)], skip=True)
                s.mm(s.ps[6][:, c0:512], s.ones_bf[:, :], E[:, pa, :W], first, last, ['ones', ('E', pa)], [('ps', 6)], skip=True)
                s.mm(s.ps[5][:, c0:512], vap, E[:, pa + 1, :W], first, last, [vkey, ('E', pa + 1)], [('ps', 5)], skip=True)
                s.mm(s.ps[7][:, c0:512], s.ones_bf[:, :], E[:, pa + 1, :W], first, last, ['ones', ('E', pa + 1)], [('ps', 7)], skip=True)
            prev = None
            cnt = 0
            loaded = {}
            if spans:
                loaded[0] = load_span(spans[0])
            for k_ in range(len(spans) + 1):
                blocks = loaded.pop(k_) if k_ < len(spans) else \
                    [(kT[:, hh, j * 128:(j + 1) * 128], ('kT', hh), vbf[:, j, hh * 128:(hh + 1) * 128], ('vbf', j), j * 128, True) for j in range(4)]
                for bj, blk in enumerate(blocks):
                    qk_exp(cnt, blk)
                    if prev is not None:
                        pv(*prev)
                    prev = (cnt, blk)
                    cnt += 1
                    if bj == 0 and k_ + 1 < len(spans):
                        loaded[k_ + 1] = load_span(spans[k_ + 1])
            pv(*prev)
            s.attn_finalize(s.ps[4][:, :], s.ps[5][:, :], s.ps[6][:, :], s.ps[7][:, :],
                            [('ps', 4), ('ps', 5), ('ps', 6), ('ps', 7)], 512, atT[:, hh, :], ('atT', hh))
        s.linear_fm('diff_out', 0, 8, 0, D, lambda kc: (atT[:, kc, :T], ('atT', kc)), T,
                    lambda m, ps, pk: s.evac_out(m, ps, pk, T))
        s.postnorm(1, 0, T, streams)
        s.sch.fence()

    def attn_sample(s, streams):
        T = 128
        P = s.P
        s.modulate(1, 0, T, streams)
        A = s.arena_bf
        qT = A[:, 0:1024].rearrange("p (c t) -> p c t", c=8)
        kT = A[:, 1024:2048].rearrange("p (c t) -> p c t", c=8)
        vbf = A[:, 2048:3072].rearrange("p (b f) -> p b f", b=1)
        atT = A[:, 3072:4096].rearrange("p (c t) -> p c t", c=8)
        E = A[:, 4096:4096 + 512].rearrange("p (a t) -> p a t", a=2)
        ktm = [A[:, 5120 + i * 1024:5120 + (i + 1) * 1024] for i in range(2)]
        vtm = [A[:, 7168 + i * 1024:7168 + (i + 1) * 1024] for i in range(2)]
        ktT = [A[:, 9216 + i * 1024:9216 + (i + 1) * 1024].rearrange("p (c t) -> p c t", c=8) for i in range(2)]
        s.memset(atT[:, :, :], 0.0, [('atT', c) for c in range(8)])
        s.attn_common_proj(T, 1, qT, kT, vbf, s.dr['o_ks'], s.dr['o_vs'], lambda tb: [(i * 32, 16, i * 16) for i in range(4)])
        nkb = P // 128
        for i in range(4):
            for kb in range(nkb + 1):
                bi = kb % 2
                if kb < nkb:
                    s.load(ktm[bi][:, :], s.dr['ck'][i, kb * 128:(kb + 1) * 128, :], ('ck', bi), [], [('ktm', bi)], q='pool')
                    s.load(vtm[bi][:, :], s.dr['cv'][i, kb * 128:(kb + 1) * 128, :], ('cvv', bi), [], [('vtm', bi)], q='pool')
                    pt = 6 + bi
                    ptb = s.ps[pt][:, :].bitcast(BF16)
                    for c in range(8):
                        s.tr(ptb[:, c * 128:(c + 1) * 128], ktm[bi][:, c * 128:(c + 1) * 128], s.ident_bf[:, :], [('ktm', bi), 'ident'], [('ps', pt)])
                    s.cp(ktT[bi][:, :, :], ptb[:, 0:1024].rearrange("p (c t) -> p c t", c=8), [('ps', pt)], [('ktT', bi)])
                    kTb = ktT[bi]
                    kkeys = [('ktT', bi)]
                    vap = vtm[bi]
                    vkeys = [('vtm', bi)]
                else:
                    kTb = kT
                    kkeys = [('kT', m) for m in range(8)]
                    vap = vbf[:, 0, :]
                    vkeys = [('vbf', 0)]
                for j in range(2):
                    pS = 2 * j + bi
                    for hh in range(8):
                        s.mm(s.ps[pS][:, hh * 16:(hh + 1) * 16], kTb[j * 64:(j + 1) * 64, hh, :], qT[j * 64:(j + 1) * 64, hh, i * 32:i * 32 + 16],
                             True, True, kkeys + [('qT', hh)], [('ps', pS)])
                for j in range(2):
                    s.act(E[:, bi, j * 128:(j + 1) * 128], s.ps[2 * j + bi][:, 0:128], AF.Exp, [('ps', 2 * j + bi)], [('E', bi)])
                if kb == nkb:
                    s.ts(E[:, bi, :], E[:, bi, :], s.rowmask_s[:, i:i + 1], None, ALU.mult, None, [('E', bi), 'rowmask_s'], [('E', bi)])
                for j in range(2):
                    for hh in range(8):
                        reg = slice(j * 128 + hh * 16, j * 128 + (hh + 1) * 16)
                        s.mm(s.ps[4][:, reg], vap[:, hh * 128:(hh + 1) * 128], E[:, bi, reg],
                             (kb == 0 and j == 0 and hh == 0), (kb == nkb), vkeys + [('E', bi)], [('ps', 4)], skip=True)
                s.mm(s.ps[5][:, 0:256], s.ones_bf[:, :], E[:, bi, :], kb == 0, kb == nkb, ['ones', ('E', bi)], [('ps', 5)])
            f = s.fin
            s.recip(f[:, 0, 0:256], s.ps[5][:, 0:256], [('ps', 5)], [('fin', 0)])
            s.tt(f[:, 0, 0:256], s.ps[4][:, 0:256], f[:, 0, 0:256], ALU.mult, [('ps', 4), ('fin', 0)], [('fin', 0)])
            s.stt(f[:, 2, 0:128], f[:, 0, 128:256], s.neglam[:, 0:1], f[:, 0, 0:128], ALU.mult, ALU.add, [('fin', 0), 'neglam'], [('fin', 2)])
            s.act(s.fin_sq[:, 0:128], f[:, 2, 0:128], AF.Square, [('fin', 2)], ['fin_sq'])
            s.rstd_from_sq(lambda c: (s.fin_sq[:, 0:128], 'fin_sq'), 1, 128, 128, f[:, 3, 0:128], ('fin', 3))
            s.tt(f[:, 2, 0:128], f[:, 2, 0:128], f[:, 3, 0:128], ALU.mult, [('fin', 2), ('fin', 3)], [('fin', 2)])
            s.act(atT[:, :, i * 32:i * 32 + 16], f[:, 2, 0:128].rearrange("p (h q) -> p h q", h=8), AF.Identity, [('fin', 2), 'subg'],
                  [('atT', c) for c in range(8)], scale=s.subg[:, 0:1])
        s.linear_fm('diff_out', 0, 8, 0, D, lambda kc: (atT[:, kc, :T], ('atT', kc)), T,
                    lambda m, ps, pk: s.evac_out(m, ps, pk, T))
        s.postnorm(1, 0, T, streams)
        s.sch.fence()

    def retention(s, T, streams, sample, st_i):
        NB = T // 128
        s.modulate(2, 0, T, streams)
        A = s.arena_bf
        off = [0]

        def take(n):
            o = off[0]
            off[0] += n
            return A[:, o:o + n]
        nvar = 4 if sample else 1
        qcT = take(8 * T).rearrange("p (c t) -> p c t", c=8)
        kpT = take(8 * T).rearrange("p (c t) -> p c t", c=8)
        vtm = take(NB * 2048).rearrange("p (b f) -> p b f", b=NB)
        onT = take(16 * T).rearrange("p (c t) -> p c t", c=16)
        kdec = take(nvar * 1024).rearrange("p (v f) -> p v f", v=nvar)
        AT = take(512).rearrange("p (h t) -> p h t", h=4)
        osq = take(512).rearrange("p (c t) -> p c t", c=4)
        qb = take(512)
        sgb = qb
        assert off[0] <= s.ABF
        Fa = s.arena_f
        cosT = Fa[:, 0:1024].rearrange("p (c t) -> p c t", c=2)
        sinT = Fa[:, 1024:2048].rearrange("p (c t) -> p c t", c=2)
        crs = Fa[:, 2048:2560].rearrange("p (h t) -> p h t", h=4)
        kiv = Fa[:, 2560:3072].rearrange("p (h t) -> p h t", h=4)
        t1 = Fa[:, 3072:3584]
        t2 = Fa[:, 3584:4096]
        ohd = Fa[:, 4096:4608].rearrange("p (c t) -> p c t", c=4)
        rs4 = Fa[:, 4608:5120].rearrange("p (h t) -> p h t", h=4)
        if sample:
            s.load(cosT[:, :, :T], s.dr['cos_s'][:, :, :], ('tb', 0), [], ['cosT'])
            s.load(sinT[:, :, :T], s.dr['sin_s'][:, :, :], ('tb', 1), [], ['sinT'])
            s.load(crs[:, :, :], s.dr['cross_s'][:, :, :], ('tb', 2), [], ['crs'])
            s.load(kiv[:, :, :], s.dr['kinv_s'][:, :, :], ('tb', 3), [], ['kiv'])
            mask = s.mask_blk_s
        else:
            t0 = st_i * 512
            s.load(cosT[:, :, :T], s.dr['cos_p'][:, :, t0:t0 + T], ('tb', 0), [], ['cosT'])
            s.load(sinT[:, :, :T], s.dr['sin_p'][:, :, t0:t0 + T], ('tb', 1), [], ['sinT'])
            s.load(crs[:, :, :], s.dr['cross_p'][:, :, 0:128], ('tb', 2), [], ['crs'])
            s.load(kiv[:, :, :], s.dr['kinv_p'][:, :, 0:128], ('tb', 3), [], ['kiv'])
            mask = s.mask_ret_p

        def ev_rot(dst, dkey, scale_tab, sc):
            def ev(m, ps, pk):
                hh = m // 2
                s.act(qb[:, :T], ps, AF.Copy, [pk], ['qb'])
                pr = s.rot()
                s.mm(s.ps[pr][:, :T], s.rotR_bf[:, :], qb[:, :T], True, True, ['rotR', 'qb'], [('ps', pr)])
                s.stt(t1[:, :T], ps, sc, cosT[:, m % 2, :T], ALU.mult, ALU.mult, [pk, 'cosT'], ['t1'])
                s.stt(t2[:, :T], s.ps[pr][:, :T], sc, sinT[:, m % 2, :T], ALU.mult, ALU.mult, [('ps', pr), 'sinT'], ['t2'])
                s.tt(t1[:, :T], t1[:, :T], t2[:, :T], ALU.add, ['t1', 't2'], ['t1'])
                for b in range(NB):
                    s.tt(dst[:, m, b * 128:(b + 1) * 128], t1[:, b * 128:(b + 1) * 128], scale_tab[:, hh, :], ALU.mult, ['t1', 'crs', 'kiv'], [(dkey, m)])
            return ev
        s.linear_fm('ret_in', 0, 8, 0, D, lambda kc: (s.hT[:, kc, :T], ('hT', kc)), T, ev_rot(qcT, 'qcT', crs, 1.0))
        s.linear_fm('ret_in', 0, 8, D, D, lambda kc: (s.hT[:, kc, :T], ('hT', kc)), T, ev_rot(kpT, 'kpT', kiv, 1.0 / 16.0))
        s.linear_tm('ret_in', 0, 8, 2 * D, 2 * D, lambda kc, tb: (s.hT[:, kc, tb * 128:(tb + 1) * 128], ('hT', kc)), NB,
                    lambda tb, b, ps, pk: s.act(vtm[:, tb, b * 512:(b + 1) * 512], ps, AF.Copy, [pk], [('vtm', tb)]))
        L = 16 if sample else 128
        rsf, rsb = s.rstate_f, s.rstate_bf
        for b in range(NB):
            cols = slice(b * 128, (b + 1) * 128)
            pt = s.rot()
            ptb = s.ps[pt][:, :].bitcast(BF16)
            for c in range(8):
                s.tr(ptb[:, c * 128:(c + 1) * 128], kpT[:, c, cols], s.ident_bf[:, :], [('kpT', c), 'ident'], [('ps', pt)])
            for v in range(nvar):
                for hh in range(4):
                    gl = RET_G[hh] ** L
                    if sample:
                        s.ts(kdec[:, v, hh * 256:(hh + 1) * 256], ptb[:, hh * 256:(hh + 1) * 256], s.rowmask_s[:, v:v + 1], gl, ALU.mult, ALU.mult,
                             [('ps', pt), 'rowmask_s'], [('kdec', v)])
                    else:
                        s.act(kdec[:, v, hh * 256:(hh + 1) * 256], ptb[:, hh * 256:(hh + 1) * 256], AF.Copy, [('ps', pt)], [('kdec', v)], scale=gl)
            pS = s.rot()
            for hh in range(4):
                for c in range(2):
                    s.mm(s.ps[pS][:, hh * 128:(hh + 1) * 128], kpT[:, 2 * hh + c, cols], qcT[:, 2 * hh + c, cols], c == 0, c == 1,
                         [('kpT', 2 * hh + c), ('qcT', 2 * hh + c)], [('ps', pS)])
            for hh in range(4):
                s.tt(AT[:, hh, :], s.ps[pS][:, hh * 128:(hh + 1) * 128], mask[:, :], ALU.mult, [('ps', pS), 'masks'], [('AT', hh)])
            for hh in range(4):
                po = 4 + hh
                for dvc in range(4):
                    s.mm(s.ps[po][:, dvc * 128:(dvc + 1) * 128], vtm[:, b, hh * 512 + dvc * 128:hh * 512 + (dvc + 1) * 128], AT[:, hh, :],
                         dvc == 0, False, [('vtm', b), ('AT', hh)], [('ps', po)], skip=True)
            for v in range(nvar):
                if sample:
                    for hh in range(4):
                        s.load(rsf[:, hh, :, :], s.dr['sret'][v, hh, :, :].rearrange("(c p) v -> p c v", p=128), ('rst', hh), [], [('rsf', hh)])
                        s.act(rsb[:, hh, :, :], rsf[:, hh, :, :], AF.Copy, [('rsf', hh)], [('rsb', hh)])
                for hh in range(4):
                    po = 4 + hh
                    for dvc in range(4):
                        if sample:
                            ccols = slice(v * 32, v * 32 + 32)
                            osub = s.ps[po][:, dvc * 128 + v * 32:dvc * 128 + v * 32 + 32]
                        else:
                            ccols = cols
                            osub = s.ps[po][:, dvc * 128:(dvc + 1) * 128]
                        for c in range(2):
                            s.mm(osub, rsb[:, hh, c, dvc * 128:(dvc + 1) * 128], qcT[:, 2 * hh + c, ccols], False, True,
                                 [('rsb', hh), ('qcT', 2 * hh + c)], [('ps', po)], skip=True)
                for hh in range(4):
                    gl = RET_G[hh] ** L
                    for c in range(2):
                        pu = s.rot()
                        s.mm(s.ps[pu][:, :], kdec[:, v, (2 * hh + c) * 128:(2 * hh + c + 1) * 128], vtm[:, b, hh * 512:(hh + 1) * 512], True, True,
                             [('kdec', v), ('vtm', b)], [('ps', pu)])
                        s.stt(rsf[:, hh, c, :], rsf[:, hh, c, :], gl, s.ps[pu][:, :], ALU.mult, ALU.add,
                              [('rsf', hh), ('ps', pu), ('rsb', hh)], [('rsf', hh)])
                    s.act(rsb[:, hh, :, :], rsf[:, hh, :, :], AF.Copy, [('rsf', hh)], [('rsb', hh)])
                    if sample:
                        s.store(s.dr['o_rs'][v, hh, :, :].rearrange("(c p) v -> p c v", p=128), rsf[:, hh, :, :], [('rsf', hh)], [])
            for hh in range(4):
                po = 4 + hh
                s.cp(ohd[:, :, :], s.ps[po][:, :].rearrange("p (c t) -> p c t", c=4), [('ps', po)], ['ohd'])
                s.act(osq[:, :, :], s.ps[po][:, :].rearrange("p (c t) -> p c t", c=4), AF.Square, [('ps', po)], ['osq'])
                s.rstd_from_sq(lambda c: (osq[:, c, :], 'osq'), 4, 512, 128, rs4[:, hh, :], ('rs4', hh))
                for dvc in range(4):
                    s.tt(onT[:, hh * 4 + dvc, cols], ohd[:, dvc, :], rs4[:, hh, :], ALU.mult, ['ohd', ('rs4', hh)], [('onT', hh * 4 + dvc)])

        def ev_g(m, ps, pk):
            s.act(sgb[:, :T], ps, AF.Silu, [pk], ['qb'])
            s.tt(onT[:, m, :T], onT[:, m, :T], sgb[:, :T], ALU.mult, [('onT', m), 'qb'], [('onT', m)])
        s.linear_fm('ret_in', 0, 8, 4 * D, 2 * D, lambda kc: (s.hT[:, kc, :T], ('hT', kc)), T, ev_g)
        s.linear_fm('ret_out', 0, 16, 0, D, lambda kc: (onT[:, kc, :T], ('onT', kc)), T,
                    lambda m, ps, pk: s.evac_out(m, ps, pk, T), blk=256)
        s.postnorm(2, 0, T, streams)
        s.sch.fence()

    def hgrn(s, T, streams, sample):
        NB = T // 128
        CL = 32 if sample else 64
        NCH = 128 // CL
        RL = 16 if sample else 64
        s.modulate(3, 0, T, streams)
        A = s.arena_bf
        off = [0]

        def take(n):
            o = off[0]
            off[0] += n
            return A[:, o:o + n]
        qd = take(8 * T).rearrange("p (c t) -> p c t", c=8)
        kiT = take(8 * T).rearrange("p (c t) -> p c t", c=8)
        vtm = take(NB * 1024).rearrange("p (b f) -> p b f", b=NB)
        onT = take(8 * T).rearrange("p (c t) -> p c t", c=8)
        kmask = take(NCH * 1024).rearrange("p (v f) -> p v f", v=NCH)
        AT = take(1024).rearrange("p (h t) -> p h t", h=8)
        osq = take(1024).rearrange("p (h t) -> p h t", h=8)
        sq_t = take(512)
        f32v = take(4096).bitcast(F32)
        ohd = f32v[:, 0:1024].rearrange("p (h t) -> p h t", h=8)
        rs8 = f32v[:, 1024:2048].rearrange("p (h t) -> p h t", h=8)
        assert off[0] <= s.ABF
        Fa = s.arena_f
        bT = Fa[:, 0:4096].rearrange("p (c t) -> p c t", c=8)
        t1a = Fa[:, 4096:4608]
        t2a = Fa[:, 4608:5120]
        ebl = Fa[:, 5120:5120 + 64].rearrange("p (c h) -> p c h", c=8)
        tb_ = take(2048).bitcast(F32)
        t1b = tb_[:, 0:512]
        t2b = tb_[:, 512:1024]
        assert off[0] <= s.ABF
        mask = s.mask_blk_s if sample else s.mask_hg_p
        rowm = s.rowmask_s if sample else s.rowmask_hp
        hsf, hsb = s.hstate_f, s.hstate_bf

        def ev_f(m, ps, pk):
            t1, t2, k1, k2 = (t1a, t2a, 't1a', 't2a') if m % 2 == 0 else (t1b, t2b, 't1b', 't2b')
            s.act(t1[:, :T], ps, AF.Sigmoid, [pk], [k1])
            s.ts(t1[:, :T], t1[:, :T], s.oml[:, m:m + 1], s.lb[:, m:m + 1], ALU.mult, ALU.add, [k1, 'lb'], [k1])
            s.act(t2[:, :T], t1[:, :T], AF.Ln, [k1], [k2])
            for ch in range(T // CL):
                s.sch.op('dve', lambda h, ch=ch, t2=t2: h.tensor_tensor_scan(bT[:, m, ch * CL:(ch + 1) * CL], s.ones_f[:, 0:CL], t2[:, ch * CL:(ch + 1) * CL],
                                                                             0.0, ALU.mult, ALU.add), [k2, 'ones_f'], [('bT', m)])
            s.ts(t1[:, :T], t1[:, :T], -1.0, 1.0, ALU.mult, ALU.add, [k1], [k1])
            s.act(t2[:, :T], bT[:, m, :T], AF.Exp, [('bT', m)], [k2], scale=-1.0)
            s.tt(kiT[:, m, :T], t1[:, :T], t2[:, :T], ALU.mult, [k1, k2], [('kiT', m)])
        s.linear_fm('hg_in', 0, 8, D, D, lambda kc: (s.hT[:, kc, :T], ('hT', kc)), T, ev_f)

        def ev_q(m, ps, pk):
            t1, t2, k1, k2 = (t1a, t2a, 't1a', 't2a') if m % 2 == 0 else (t1b, t2b, 't1b', 't2b')
            s.act(t1[:, :T], ps, AF.Silu, [pk], [k1])
            s.act(t2[:, :T], bT[:, m, :T], AF.Exp, [('bT', m)], [k2])
            s.tt(qd[:, m, :T], t1[:, :T], t2[:, :T], ALU.mult, [k1, k2], [('qd', m)])
        s.linear_fm('hg_in', 0, 8, 0, D, lambda kc: (s.hT[:, kc, :T], ('hT', kc)), T, ev_q)
        s.linear_tm('hg_in', 0, 8, 2 * D, D, lambda kc, tb: (s.hT[:, kc, tb * 128:(tb + 1) * 128], ('hT', kc)), NB,
                    lambda tb, b, ps, pk: s.act(vtm[:, tb, b * 512:(b + 1) * 512], ps, AF.Copy, [pk], [('vtm', tb)]))
        for b in range(NB):
            cols = slice(b * 128, (b + 1) * 128)
            pt = s.rot()
            ptb = s.ps[pt][:, :].bitcast(BF16)
            for c in range(8):
                s.tr(ptb[:, c * 128:(c + 1) * 128], kiT[:, c, cols], s.ident_bf[:, :], [('kiT', c), 'ident'], [('ps', pt)])
            for v in range(NCH):
                s.ts(kmask[:, v, :], ptb[:, 0:1024], rowm[:, v:v + 1], None, ALU.mult, None, [('ps', pt), 'rowmask'], [('kmask', v)])
            for v in range(NCH):
                s.act(ebl[:, v, :], bT[:, :, b * 128 + v * CL + RL - 1], AF.Exp, [('bT', m) for m in range(8)], [('ebl', v)])
            pS = [s.rot(), s.rot()]
            for hh in range(8):
                s.mm(s.ps[pS[hh // 4]][:, (hh % 4) * 128:(hh % 4 + 1) * 128], kiT[:, hh, cols], qd[:, hh, cols], True, True,
                     [('kiT', hh), ('qd', hh)], [('ps', pS[hh // 4])])
            for hh in range(8):
                s.tt(AT[:, hh, :], s.ps[pS[hh // 4]][:, (hh % 4) * 128:(hh % 4 + 1) * 128], mask[:, :], ALU.mult, [('ps', pS[hh // 4]), 'masks'], [('AT', hh)])
            for hh in range(8):
                po = 4 + hh // 4
                s.mm(s.ps[po][:, (hh % 4) * 128:(hh % 4 + 1) * 128], vtm[:, b, hh * 128:(hh + 1) * 128], AT[:, hh, :], hh % 4 == 0, False,
                     [('vtm', b), ('AT', hh)], [('ps', po)], skip=True)
            for v in range(NCH):
                if sample:
                    s.load(hsf[:, :, :], s.dr['shg'][v, :, :, :].rearrange("h k v -> k h v"), ('hst', 0), [], ['hsf'])
                    s.act(hsb[:, :, :], hsf[:, :, :], AF.Copy, ['hsf'], ['hsb'])
                for hh in range(8):
                    po = 4 + hh // 4
                    osub = s.ps[po][:, (hh % 4) * 128 + v * CL:(hh % 4) * 128 + (v + 1) * CL]
                    s.mm(osub, hsb[:, hh, :], qd[:, hh, b * 128 + v * CL:b * 128 + (v + 1) * CL], False, True,
                         ['hsb', ('qd', hh)], [('ps', po)], skip=True)
                for half in range(2):
                    pu = s.rot()
                    for h4 in range(4):
                        hh = half * 4 + h4
                        s.mm(s.ps[pu][:, h4 * 128:(h4 + 1) * 128], kmask[:, v, hh * 128:(hh + 1) * 128], vtm[:, b, hh * 128:(hh + 1) * 128], h4 == 0, True,
                             [('kmask', v), ('vtm', b)], [('ps', pu)], skip=True)
                    hs = hsf[:, half * 4:half * 4 + 4, :]
                    s.tt(hs, hs, s.ps[pu][:, :].rearrange("p (h d) -> p h d", h=4), ALU.add, ['hsf', ('ps', pu), 'hsb'], ['hsf'])
                for hh in range(8):
                    s.ts(hsf[:, hh, :], hsf[:, hh, :], ebl[:, v, hh:hh + 1], None, ALU.mult, None, ['hsf', ('ebl', v)], ['hsf'])
                s.act(hsb[:, :, :], hsf[:, :, :], AF.Copy, ['hsf'], ['hsb'])
                if sample:
                    s.store(s.dr['o_hs'][v, :, :, :].rearrange("h k v -> k h v"), hsf[:, :, :], ['hsf'], [])
            for half in range(2):
                po = 4 + half
                s.cp(ohd[:, half * 4:half * 4 + 4, :], s.ps[po][:, :].rearrange("p (h t) -> p h t", h=4), [('ps', po)], ['ohd'])
                s.act(osq[:, half * 4:half * 4 + 4, :], s.ps[po][:, :].rearrange("p (h t) -> p h t", h=4), AF.Square, [('ps', po)], ['osq'])
            for hh in range(8):
                s.rstd_from_sq(lambda c, hh=hh: (osq[:, hh, :], 'osq'), 1, 128, 128, rs8[:, hh, :], ('rs8', hh))
                s.stt(onT[:, hh, cols], ohd[:, hh, :], s.hgn[:, hh:hh + 1], rs8[:, hh, :], ALU.mult, ALU.mult, ['ohd', ('rs8', hh), 'hgn'], [('onT', hh)])

        def ev_g(m, ps, pk):
            s.act(sq_t[:, :T], ps, AF.Silu, [pk], ['sq_t'])
            s.tt(onT[:, m, :T], onT[:, m, :T], sq_t[:, :T], ALU.mult, [('onT', m), 'sq_t'], [('onT', m)])
        s.linear_fm('hg_in', 0, 8, 3 * D, D, lambda kc: (s.hT[:, kc, :T], ('hT', kc)), T, ev_g)
        s.linear_fm('hg_out', 0, 8, 0, D, lambda kc: (onT[:, kc, :T], ('onT', kc)), T,
                    lambda m, ps, pk: s.evac_out(m, ps, pk, T))
        s.postnorm(3, 0, T, streams)
        s.sch.fence()

    def load_x(s, src_rows, T):
        NB = T // 128
        for tb in range(NB):
            st = s.stage[tb % 2]
            sk = ('stage', tb % 2)
            for (ap_, p0, n, tb_) in src_rows:
                if tb_ == tb:
                    s.load(st[p0:p0 + n, :], ap_, ('xl', tb % 2), [], [sk])
            for half in range(2):
                pi = s.rot()
                for c4 in range(4):
                    c = half * 4 + c4
                    s.tr(s.ps[pi][:, c4 * 128:(c4 + 1) * 128], st[:, c * 128:(c + 1) * 128], s.ident_f[:, :], [sk, 'ident'], [('ps', pi)])
                s.cp(s.xT[:, half * 4:half * 4 + 4, tb * 128:(tb + 1) * 128], s.ps[pi][:, :].rearrange("p (c t) -> p c t", c=4), [('ps', pi)], ['xT'])

    def store_y(s, dst_rows, T):
        NB = T // 128
        for tb in range(NB):
            st = s.stage[tb % 2]
            sk = ('stage', tb % 2)
            for half in range(2):
                pi = s.rot()
                for c4 in range(4):
                    c = half * 4 + c4
                    s.tr(s.ps[pi][:, c4 * 128:(c4 + 1) * 128], s.xT[:, c, tb * 128:(tb + 1) * 128], s.ident_f[:, :], ['xT', 'ident'], [('ps', pi)])
                s.cp(st[:, half * 512:(half + 1) * 512], s.ps[pi][:, :], [('ps', pi)], [sk])
            for (ap_, p0, n, tb_) in dst_rows:
                if tb_ == tb:
                    s.store(ap_, st[p0:p0 + n, :], [sk], [])

    def load_rows_fm(s, rows_aps, dst, dkey):
        st = s.stage[0]
        sk = ('stage', 0)
        r = 0
        s.memset(st[:, 0:128], 0.0, [sk])
        for ap_ in rows_aps:
            n = ap_.shape[0]
            s.load(st[r:r + n, 0:128], ap_, ('xl', 0), [], [sk])
            r += n
        assert r <= 128
        pi = s.rot()
        s.tr(s.ps[pi][:, 0:128], st[:, 0:128], s.ident_f[:, :], [sk, 'ident'], [('ps', pi)])
        s.cp(dst, s.ps[pi][:, 0:r], [('ps', pi)], [dkey])

    def build(s):
        nc = s.nc
        S, P = s.S_len, s.P
        NST = S // 512
        s.din('xp', [S, D]); s.din('xs', [128, D]); s.din('ck', [4, P, D]); s.din('cv', [4, P, D])
        s.din('sret', [4, 4, 256, 512]); s.din('shg', [4, 8, 128, 128]); s.din('cvec', [5, D]); s.din('pmask', [128, 1])
        s.din('w_ada', [4 * D, 6 * D]); s.din('b_ada', [4 * 48, 128]); s.din('norm_gains', [128, 128])
        s.din('gmlp_w_in', [D, 2 * GH]); s.din('gmlp_ln_g', [1, GH]); s.din('gmlp_ln_b', [1, GH]); s.din('gmlp_ln_g_fm', [24, 128])
        s.din('gmlp_w_s', [8, 128, 128]); s.din('gmlp_b_s', [8, 128]); s.din('gmlp_w_out', [GH, D])
        s.din('diff_w_in', [D, 3 * D]); s.din('diff_lambda', [1, 256]); s.din('diff_subln', [128, 1]); s.din('diff_w_out', [D, D])
        s.din('ret_w_in', [D, 6 * D]); s.din('ret_w_out', [2 * D, D])
        s.din('hgrn_w_in', [D, 4 * D]); s.din('hgrn_norm', [8, 128]); s.din('hgrn_w_out', [D, D]); s.din('hgrn_lb', [32, 128])
        s.din('ffn_w_in', [4 * D, 2 * FH]); s.din('ffn_w_out', [4 * FH, D])
        cshapes = {k: v.shape for k, v in make_consts(S, P).items()}
        for k, shp in cshapes.items():
            s.din(k, list(shp))
        s.dout('o_yp', [S, D]); s.dout('o_ys', [64, D]); s.dout('o_gv', [64, GH]); s.dout('o_kp', [S, D]); s.dout('o_vp', [S, D])
        s.dout('o_ks', [64, D]); s.dout('o_vs', [64, D]); s.dout('o_rp', [4, 256, 512]); s.dout('o_rs', [4, 4, 256, 512])
        s.dout('o_hp', [8, 128, 128]); s.dout('o_hs', [4, 8, 128, 128])
        s.dint('kscr', [8, 128, S]); s.dint('vscr', [8, 128, S // 128, 128])
        s.xT = s.sb('xT', [128, 8, 512]); s.hT = s.sb('hT', [128, 8, 512], BF16); s.outT = s.sb('outT', [128, 8, 512], BF16)
        s.sqT = s.sb('sqT', [128, 8, 512], BF16)
        s.wslot = [s.sb('wslot%d' % i, [128, SLOT_EL], BF16) for i in range(NSLOT)]
        s.ABF = 27136
        s.arena_bf = s.sb('arena_bf', [128, s.ABF], BF16)
        s.arena_f = s.sb('arena_f', [128, 5248], F32)
        s.big_f = s.arena_f
        s.stage = [s.sb('stage%d' % i, [128, 1024]) for i in range(2)]
        s.rstate_f = s.sb('rsf', [128, 4, 2, 512]); s.rstate_bf = s.sb('rsb', [128, 4, 2, 512], BF16)
        s.hstate_f = s.sb('hsf', [128, 8, 128]); s.hstate_bf = s.sb('hsb', [128, 8, 128], BF16)
        s.rstd = s.sb('rstd', [128, 512]); s.tmpf = s.sb('tmpf', [128, 512]); s.rs_tmp = s.tmpf
        s.fin = s.arena_f[:, 0:2048].rearrange("p (a t) -> p a t", a=4); s.fin_sq = s.sb('fin_sq', [128, 512], BF16)
        s.ident_f = s.sb('ident_f', [128, 128]); s.ident_bf = s.sb('ident_bf', [128, 128], BF16)
        s.ones_bf = s.sb('ones_bf', [128, 128], BF16); s.ones_f = s.sb('ones_f', [128, 64])
        s.rotR_bf = s.sb('rotR_bf', [128, 128], BF16)
        s.eps_t = s.sb('eps_t', [128, 1])
        s.comb = s.sb('comb', [128, 4, 6, 8, 5]); s.modraw = s.arena_f[:, 4096:4096 + 240].rearrange("p (j s) -> p j s", j=48)
        s.bada = s.sb('bada', [128, 192]); s.ng = s.sb('ng', [128, 128]); s.cT = s.sb('cT', [128, 40]); s.cT_bf = s.sb('cT_bf', [128, 8, 5], BF16)
        s.lng_fm = s.sb('lng_fm', [128, 24]); s.hgn = s.sb('hgn', [128, 8]); s.hlbT = s.sb('hlbT', [128, 32])
        s.lb = s.sb('lb', [128, 8]); s.oml = s.sb('oml', [128, 8]); s.subg = s.sb('subg', [128, 1]); s.neglam = s.sb('neglam', [128, 1])
        s.lrow = s.sb('lrow', [2, GH], BF16); s.lrow_f = s.arena_bf[0:2, 0:4 * GH].bitcast(F32).rearrange("p (a f) -> p a f", a=2)
        s.rows2_p = s.sb('rows2_p', [2, 8, 128], BF16); s.rows2_s = s.sb('rows2_s', [2, 8, 128], BF16)
        s.rows2_f = s.arena_f[0:2, 0:1024].rearrange("p (g t) -> p g t", g=8)
        s.spw_p = s.sb('spw_p', [128, 8, 128], BF16); s.spw_s = s.sb('spw_s', [128, 8, 128], BF16)
        s.mask_ret_p = s.sb('mask_ret_p', [128, 128]); s.mask_blk_s = s.sb('mask_blk_s', [128, 128]); s.mask_hg_p = s.sb('mask_hg_p', [128, 128])
        s.tril = s.sb('tril', [128, 128])
        s.rowmask_s = s.sb('rowmask_s', [128, 4]); s.rowmask_hp = s.sb('rowmask_hp', [128, 2])
        s.bnst = s.sb('bnst', [128, 6, 6]); s.bnag = s.sb('bnag', [128, 2])
        s.col_a = s.sb('col_a', [128, 1]); s.col_b = s.sb('col_b', [128, 1]); s.col_c = s.sb('col_c', [128, 1])
        s.lam_t = s.sb('lam_t', [1, 256]); s.lam_s = s.sb('lam_s', [1, 8]); s.pmask = s.sb('pmask_t', [128, 1])
        s.ps = [s.es.enter_context(nc.psum_tensor('ps%d' % i, [128, 512], F32)) for i in range(8)]
        s.wb = {}
        s.wmeta = {}
        s.cv_i = 0
        s.span_i = 0
        import os as _os
        s.SPB = int(_os.environ.get('KSPB', '16'))
        dr = s.dr

        def cv_layer(l):
            s.convert_weight('w_ada', dr['w_ada'], 4 * D, 6 * D, 8, 512, layers=[l])
            if l == 0:
                s.convert_weight('gmlp_in', dr['gmlp_w_in'], D, 2 * GH, 8, 512)
                s.convert_weight('gmlp_out', dr['gmlp_w_out'], GH, D, 24, 128)
            elif l == 1:
                s.convert_weight('diff_in', dr['diff_w_in'], D, 3 * D, 8, 512)
                s.convert_weight('diff_out', dr['diff_w_out'], D, D, 8, 512)
            elif l == 2:
                s.convert_weight('ret_in', dr['ret_w_in'], D, 6 * D, 8, 512)
                s.convert_weight('ret_out', dr['ret_w_out'], 2 * D, D, 16, 256)
            else:
                s.convert_weight('hg_in', dr['hgrn_w_in'], D, 4 * D, 8, 512)
                s.convert_weight('hg_out', dr['hgrn_w_out'], D, D, 8, 512)
            s.convert_weight('ffn_in', dr['ffn_w_in'], 4 * D, 2 * FH, 8, 256, ffn_pair=True, layers=[l])
            s.convert_weight('ffn_out', dr['ffn_w_out'], 4 * FH, D, 22, 128, layers=[l])
        for l in range(4):
            cv_layer(l)

        if getattr(s, 'KS', 99) < 1:
            return
        s.load(s.ident_f[:, :], dr['ident'], ('c', 0), [], ['ident_f'])
        s.cp(s.ident_bf[:, :], s.ident_f[:, :], ['ident_f'], ['ident'])
        s.load(s.stage[1][:, 0:128], dr['rotR'], ('c', 1), [], [('stage', 1)])
        s.cp(s.rotR_bf[:, :], s.stage[1][:, 0:128], [('stage', 1)], ['rotR'])
        s.memset(s.ones_bf[:, :], 1.0, ['ones']); s.memset(s.ones_f[:, :], 1.0, ['ones_f']); s.memset(s.eps_t[:, :], EPS, ['eps'])
        s.load(s.mask_ret_p[:, :], dr['mask_ret_p'], ('c', 2), [], ['masks'])
        s.load(s.mask_blk_s[:, :], dr['mask_blk_s'], ('c', 3), [], ['masks'])
        s.load(s.mask_hg_p[:, :], dr['mask_hg_p'], ('c', 4), [], ['masks'])
        s.load(s.tril[:, :], dr['tril'], ('c', 5), [], ['tril'])
        s.load(s.rowmask_s[:, :], dr['rowmask_s'], ('c', 6), [], ['rowmask_s', 'rowmask'])
        s.load(s.rowmask_hp[:, :], dr['rowmask_hp'], ('c', 7), [], ['rowmask'])
        s.load(s.subg[:, :], dr['diff_subln'], ('c', 0), [], ['subg'])
        s.ts(s.subg[:, :], s.subg[:, :], 1.0 - LAM_INIT, None, ALU.mult, None, ['subg'], ['subg'])
        s.sch.lastw['ident'] = s.sch.lastw['ident']
        s.load_rows_fm([dr['b_ada'][0:128, :]], s.bada[:, 0:128], 'bada')
        s.load_rows_fm([dr['b_ada'][128:192, :]], s.bada[:, 128:192], 'bada')
        s.load_rows_fm([dr['norm_gains']], s.ng[:, :], 'ng')
        s.load_rows_fm([dr['cvec'].rearrange("s (c p) -> (s c) p", p=128)], s.cT[:, :], 'cT')
        s.load_rows_fm([dr['gmlp_ln_g_fm']], s.lng_fm[:, :], 'lng_fm')
        s.load_rows_fm([dr['hgrn_norm']], s.hgn[:, :], 'hgn')
        s.load_rows_fm([dr['hgrn_lb']], s.hlbT[:, :], 'hlbT')
        s.act(s.cT_bf[:, :, :], s.cT[:, :].rearrange("p (s c) -> p c s", s=5), AF.Silu, ['cT'], ['cT_bf'])
        hv = s.hlbT[:, :].rearrange("p (l c) -> p l c", l=4)
        s.act(s.hlbT[:, :], s.hlbT[:, :], AF.Exp, ['hlbT'], ['hlbT'])
        s.tt(s.lb[:, :], hv[:, 0, :], hv[:, 1, :], ALU.add, ['hlbT'], ['lb'])
        s.tt(s.lb[:, :], s.lb[:, :], hv[:, 2, :], ALU.add, ['hlbT', 'lb'], ['lb'])
        s.tt(s.oml[:, :], s.lb[:, :], hv[:, 3, :], ALU.add, ['hlbT', 'lb'], ['oml'])
        s.recip(s.oml[:, :], s.oml[:, :], ['oml'], ['oml'])
        s.tt(s.oml[:, :], s.oml[:, :], hv[:, 0, :], ALU.mult, ['oml', 'hlbT'], ['oml'])
        s.ts(s.lb[:, :], s.oml[:, :], -1.0, 1.0, ALU.mult, ALU.add, ['oml'], ['lb'])
        s.load(s.lam_t[:, :], dr['diff_lambda'], ('c', 1), [], ['lam_t'])
        s.tt(s.lam_t[:, 0:64], s.lam_t[:, 0:64], s.lam_t[:, 64:128], ALU.mult, ['lam_t'], ['lam_t'])
        s.tt(s.lam_t[:, 128:192], s.lam_t[:, 128:192], s.lam_t[:, 192:256], ALU.mult, ['lam_t'], ['lam_t'])
        s.sch.op('dve', lambda h: h.tensor_reduce(s.lam_s[:, 0:1], s.lam_t[:, 0:64], mybir.AxisListType.X, ALU.add), ['lam_t'], ['lam_s'])
        s.sch.op('dve', lambda h: h.tensor_reduce(s.lam_s[:, 1:2], s.lam_t[:, 128:192], mybir.AxisListType.X, ALU.add), ['lam_t', 'lam_s'], ['lam_s'])
        s.act(s.lam_s[:, 2:4], s.lam_s[:, 0:2], AF.Exp, ['lam_s'], ['lam_s'])
        s.tt(s.lam_s[:, 4:5], s.lam_s[:, 3:4], s.lam_s[:, 2:3], ALU.subtract, ['lam_s'], ['lam_s'])
        s.ts(s.lam_s[:, 5:6], s.lam_s[:, 4:5], -LAM_INIT, None, ALU.add, None, ['lam_s'], ['lam_s'])
        orow = s.onesrow_f()
        pi = s.rot()
        s.mm(s.ps[pi][:, 0:2], orow, s.lam_s[0:1, 4:6], True, True, ['lam_s', 'onesrow'], [('ps', pi)])
        s.cp(s.neglam[:, :], s.ps[pi][:, 1:2], [('ps', pi)], ['neglam'])

    def onesrow_f(s):
        if not hasattr(s, '_onesrow'):
            s._onesrow = s.sb('onesrow', [1, 128])
            s.memset(s._onesrow[:, :], 1.0, ['onesrow'])
        return s._onesrow[0:1, :]

    def adaln(s, layers):
        dr = s.dr
        if 0 in layers:
            s.load(s.pmask[:, :], dr['pmask'], ('c', 0), [], ['pmask'])
        for l in layers:
            done = 0
            j = 0
            while done < 6 * D:
                wv, wkey = s.wload('w_ada', l * D, 8, done, 512)
                pi = s.rot()
                for m_ in range(4):
                    for kc in range(8):
                        s.mm(s.ps[pi][:, m_ * 8:m_ * 8 + 5], wv[:, kc, m_ * 128:(m_ + 1) * 128], s.cT_bf[:, kc, :], kc == 0, kc == 7,
                             [wkey, 'cT_bf'], [('ps', pi)])
                for m_ in range(4):
                    s.act(s.modraw[:, j, :], s.ps[pi][:, m_ * 8:m_ * 8 + 5], AF.Identity, [('ps', pi), 'bada'], ['modraw'],
                          bias=s.bada[:, l * 48 + j:l * 48 + j + 1])
                    j += 1
                done += 512
            for which in range(2):
                base = which * 24
                for si in range(5):
                    s.cp(s.comb[:, l, 3 * which + 0, :, si], s.modraw[:, base:base + 8, si], ['modraw'], ['comb'])
                    s.stt(s.comb[:, l, 3 * which + 1, :, si], s.modraw[:, base + 8:base + 16, si], 1.0, s.ng[:, l * 32 + which * 16:l * 32 + which * 16 + 8],
                          ALU.add, ALU.mult, ['modraw', 'ng'], ['comb'])
                    s.tt(s.comb[:, l, 3 * which + 2, :, si], s.modraw[:, base + 16:base + 24, si],
                         s.ng[:, l * 32 + which * 16 + 8:l * 32 + which * 16 + 16], ALU.mult, ['modraw', 'ng'], ['comb'])
                for kind in (0, 1):
                    s.ts(s.comb[:, l, 3 * which + kind, :, 0], s.comb[:, l, 3 * which + kind, :, 0], s.pmask[:, 0:1], None, ALU.mult, None,
                         ['comb', 'pmask'], ['comb'])

    def gmlp_setup(s):
        dr = s.dr
        s.load(s.lrow_f[0:1, 0, :], dr['gmlp_ln_g'], ('c', 2), [], ['lrow_f'])
        s.load(s.lrow_f[1:2, 0, :], dr['gmlp_ln_g'], ('c', 3), [], ['lrow_f'])
        s.memset(s.lrow_f[:, 1, :], 1.0, ['lrow_f'])
        s.load(s.lrow_f[0:1, 1, :], dr['gmlp_ln_b'], ('c', 4), [], ['lrow_f'])
        s.recip(s.lrow_f[:, 0, :], s.lrow_f[:, 0, :], ['lrow_f'], ['lrow_f'])
        s.tt(s.lrow[:, :], s.lrow_f[:, 0, :], s.lrow_f[:, 1, :], ALU.mult, ['lrow_f'], ['lrow'])
        for sample in (False, True):
            spw = s.spw_s if sample else s.spw_p
            rows2 = s.rows2_s if sample else s.rows2_p
            T = 128 if sample else 512
            s.memset(s.rows2_f[:, :, :], 0.0, ['rows2_f'])
            for g in range(8):
                st = s.stage[g % 2]
                sk = ('stage', g % 2)
                if sample:
                    s.memset(st[:, 0:128], 0.0, [sk])
                    for i in range(4):
                        s.load(st[i * 32:i * 32 + 16, i * 32:i * 32 + 16], dr['gmlp_w_s'][g, 0:16, 0:16], ('xl', g % 2), [], [sk])
                        s.load(s.rows2_f[1:2, g, i * 32:i * 32 + 16], dr['gmlp_b_s'][g:g + 1, 0:16], ('c', 5), [], ['rows2_f'])
                else:
                    s.load(st[:, 0:128], dr['gmlp_w_s'][g, :, :], ('xl', g % 2), [], [sk])
                    s.load(s.rows2_f[1:2, g, :], dr['gmlp_b_s'][g:g + 1, :], ('c', 5), [], ['rows2_f'])
                s.tt(st[:, 0:128], st[:, 0:128], s.tril[:, :], ALU.mult, [sk, 'tril'], [sk])
                pi = s.rot()
                s.tr(s.ps[pi][:, 0:128], st[:, 0:128], s.ident_f[:, :], [sk, 'ident'], [('ps', pi)])
                s.cp(spw[:, g, :], s.ps[pi][:, 0:128], [('ps', pi)], ['spw'])
                pj = s.rot()
                s.mm(s.ps[pj][0:1, 0:128], s.ones_bf[:, 0:1], spw[:, g, :], True, True, ['ones', 'spw'], [('ps', pj)])
                s.cp(s.rows2_f[0:1, g, :], s.ps[pj][0:1, 0:128], [('ps', pj)], ['rows2_f'])
            s.cp(rows2[:, :, :], s.rows2_f[:, :, :], ['rows2_f'], ['rows2'])

    def program(s):
        S, P = s.S_len, s.P
        NST = S // 512
        dr = s.dr
        import os
        KS = int(os.environ.get('KSTOP', '99'))
        s.KS = KS
        s.build()
        if KS >= 2:
            s.adaln([0])
        if KS >= 3:
            s.gmlp_setup()
        s.sch.fence()
        if KS < 4:
            s.sch.finish(); s.sch.emit(s.nc, s.es); s.es.close(); return s.nc
        s.memset(s.rstate_f[:, :, :, :], 0.0, [('rsf', h) for h in range(4)])
        s.memset(s.rstate_bf[:, :, :, :], 0.0, [('rsb', h) for h in range(4)])
        s.memset(s.hstate_f[:, :, :], 0.0, ['hsf'])
        s.memset(s.hstate_bf[:, :, :], 0.0, ['hsb'])
        pst = [(0, 512, 0)]
        for st_i in range(NST):
            t0 = st_i * 512
            s.load_x([(dr['xp'][t0 + tb * 128:t0 + (tb + 1) * 128, :], 0, 128, tb) for tb in range(4)], 512)
            if KS >= 5: s.gmlp(512, pst, False)
            if KS >= 6: s.ffn(0, 512, pst)
            if st_i == 0:
                s.adaln([1]); s.sch.fence()
            if KS >= 7: s.attn_prompt(st_i)
            if KS >= 8: s.ffn(1, 512, pst)
            if st_i == 0:
                s.adaln([2]); s.sch.fence()
            if KS >= 9: s.retention(512, pst, False, st_i)
            if KS >= 10: s.ffn(2, 512, pst)
            if st_i == 0:
                s.adaln([3]); s.sch.fence()
            if KS >= 11: s.hgrn(512, pst, False)
            if KS >= 12: s.ffn(3, 512, pst)
            s.store_y([(dr['o_yp'][t0 + tb * 128:t0 + (tb + 1) * 128, :], 0, 128, tb) for tb in range(4)], 512)
            s.sch.fence()
        for hh in range(4):
            s.store(dr['o_rp'][hh, :, :].rearrange("(c p) v -> p c v", p=128), s.rstate_f[:, hh, :, :], [('rsf', hh)], [])
        s.store(dr['o_hp'].rearrange("h k v -> k h v"), s.hstate_f[:, :, :], [('hsf')], [])
        s.sch.fence()
        if KS < 13:
            s.sch.finish(); s.sch.emit(s.nc, s.es); s.es.close(); return s.nc
        sst = [(i * 32, (i + 1) * 32, 1 + i) for i in range(4)]
        s.load_x([(dr['xs'][:, :], 0, 128, 0)], 128)
        af2 = s.arena_bf[:, 8192:8192 + 4 * GH].bitcast(F32)
        s.lng_b = af2[:, 0:GH]
        s.lnb_b = af2[:, GH:2 * GH]
        s.load(s.lng_b, dr['gmlp_ln_g'].partition_broadcast(128), ('c', 6), [], ['lngb'])
        s.load(s.lnb_b, dr['gmlp_ln_b'].partition_broadcast(128), ('c', 7), [], ['lngb'])
        if KS >= 14: s.gmlp(128, sst, True)
        if KS >= 15: s.ffn(0, 128, sst)
        if KS >= 16: s.attn_sample(sst)
        if KS >= 17: s.ffn(1, 128, sst)
        if KS >= 18: s.retention(128, sst, True, 0)
        if KS >= 19: s.ffn(2, 128, sst)
        if KS >= 20: s.hgrn(128, sst, True)
        if KS >= 21: s.ffn(3, 128, sst)
        s.store_y([(dr['o_ys'][i * 16:(i + 1) * 16, :], i * 32, 16, 0) for i in range(4)], 128)
        s.sch.finish()
        s.sch.emit(s.nc, s.es)
        s.es.close()
        return s.nc


_CACHE = {}


def kernel(**inp):
    S = inp['x_prompt'].shape[1]
    P = inp['cache_k_diff'].shape[2]
    key = (S, P)
    if key not in _CACHE:
        _CACHE[key] = Builder(S, P).program()
    nc = _CACHE[key]
    consts = make_consts(S, P)
    f = lambda a: np.ascontiguousarray(np.asarray(a, dtype=np.float32))
    shared = {
        'w_ada': f(inp['w_ada']).reshape(4 * D, 6 * D),
        'b_ada': f(inp['b_ada']).reshape(4 * 48, 128),
        'norm_gains': f(inp['norm_gains']).reshape(128, 128),
        'gmlp_w_in': f(inp['gmlp_w_in'][0]), 'gmlp_ln_g': f(inp['gmlp_ln_g'][0]).reshape(1, GH), 'gmlp_ln_b': f(inp['gmlp_ln_b'][0]).reshape(1, GH),
        'gmlp_ln_g_fm': f(inp['gmlp_ln_g'][0]).reshape(24, 128),
        'gmlp_w_s': f(inp['gmlp_w_s'][0]), 'gmlp_b_s': f(inp['gmlp_b_s'][0]), 'gmlp_w_out': f(inp['gmlp_w_out'][0]),
        'diff_w_in': f(inp['diff_w_in'][0]), 'diff_lambda': f(inp['diff_lambda'][0]).reshape(1, 256),
        'diff_subln': f(inp['diff_subln'][0]).reshape(128, 1), 'diff_w_out': f(inp['diff_w_out'][0]),
        'ret_w_in': f(inp['ret_w_in'][0]), 'ret_w_out': f(inp['ret_w_out'][0]),
        'hgrn_w_in': f(inp['hgrn_w_in'][0]), 'hgrn_norm': f(inp['hgrn_norm'][0]).reshape(8, 128), 'hgrn_w_out': f(inp['hgrn_w_out'][0]),
        'hgrn_lb': f(inp['hgrn_lower_bounds']).reshape(32, 128),
        'ffn_w_in': f(inp['ffn_w_in']).reshape(4 * D, 2 * FH), 'ffn_w_out': f(inp['ffn_w_out']).reshape(4 * FH, D),
    }
    shared.update(consts)
    xp, xs = f(inp['x_prompt']), f(inp['x_sample'])
    ck, cv = inp['cache_k_diff'][0], inp['cache_v_diff'][0]
    in_maps = []
    OWN = [0, 1, 4, 5]
    for c in range(8):
        b = OWN.index(c) if c in OWN else 0
        xs_pad = np.zeros((128, D), np.float32)
        for i in range(4):
            xs_pad[i * 32:i * 32 + 16] = xs[4 * c + i]
        m = dict(shared)
        m['xp'] = xp[b] if c in OWN else np.zeros_like(xp[0])
        m['pmask'] = np.full((128, 1), 1.0 if c in OWN else 0.0, np.float32)
        m['xs'] = xs_pad
        m['ck'] = f(ck[4 * c:4 * c + 4]).reshape(4, P, D)
        m['cv'] = f(cv[4 * c:4 * c + 4]).reshape(4, P, D)
        m['sret'] = f(inp['state_retention'][0, 4 * c:4 * c + 4])
        m['shg'] = f(inp['state_hgrn'][0, 4 * c:4 * c + 4])
        m['cvec'] = np.concatenate([f(inp['c_prompt'])[b:b + 1], f(inp['c_sample'])[4 * c:4 * c + 4]], axis=0)
        in_maps.append(m)
    import os
    ncores = int(os.environ.get('KCORES', '8'))
    res = run_bass_kernel_spmd(nc, in_maps[:ncores], core_ids=list(range(ncores))).results
    res = list(res) + [res[0]] * (8 - ncores)
    B = xp.shape[0]
    yp = np.stack([res[OWN[b]]['o_yp'] for b in range(B)])
    ys = np.concatenate([res[c]['o_ys'].reshape(4, 16, D) for c in range(8)])
    gv = np.concatenate([res[c]['o_gv'].reshape(4, 16, GH) for c in range(8)])[None]
    kp = np.stack([res[OWN[b]]['o_kp'].reshape(S, 16, 64) for b in range(B)])[None]
    vp = np.stack([res[OWN[b]]['o_vp'].reshape(S, 8, 128) for b in range(B)])[None]
    ks = np.concatenate([res[c]['o_ks'].reshape(4, 16, 16, 64) for c in range(8)])[None]
    vs = np.concatenate([res[c]['o_vs'].reshape(4, 16, 8, 128) for c in range(8)])[None]
    rp = np.stack([res[OWN[b]]['o_rp'] for b in range(B)])[None]
    rs = np.concatenate([res[c]['o_rs'] for c in range(8)])[None]
    hp = np.stack([res[OWN[b]]['o_hp'] for b in range(B)])[None]
    hs = np.concatenate([res[c]['o_hs'] for c in range(8)])[None]
    return tuple(np.ascontiguousarray(a, dtype=np.float32) for a in (yp, ys, gv, kp, vp, ks, vs, rp, rs, hp, hs))
```

```python
import math
import numpy as np
import concourse.bass as bass
import concourse.mybir as mybir
from concourse.bass_utils import run_bass_kernel_spmd
from contextlib import ExitStack

F32 = mybir.dt.float32
BF16 = mybir.dt.bfloat16
AF = mybir.ActivationFunctionType
ALU = mybir.AluOpType

D = 1024
FH = 2816
GH = 3072
EPS = 1e-6
ENGS = ['pe', 'act', 'dve', 'pool', 'sp']
EPOCH = 20000
NSLOT = 3
SLOT_EL = 4096
LAM_INIT = 0.8 - 0.6 * math.exp(-0.3 * 1)
RET_G = [1.0 - 2.0 ** (-5.0 - h) for h in range(4)]


class Sched:
    def __init__(s):
        s.ops = {e: [] for e in ENGS}
        s.lastw = {}
        s.readers = {}
        s.chan_n = {}

    def _deps(s, eng, reads, writes):
        d = {}

        def need(tok):
            if tok[0] == 'e':
                _, e, i = tok
                if e == eng and e == 'pe':
                    return
                d[('e', e)] = max(d.get(('e', e), -1), i)
            else:
                _, c, v = tok
                d[('d', c)] = max(d.get(('d', c), -1), v)
        for k in reads:
            if k in s.lastw:
                need(s.lastw[k])
        for k in writes:
            if k in s.lastw:
                need(s.lastw[k])
            for tok in s.readers.get(k, {}).values():
                need(tok)
        return d

    def op(s, eng, fn, reads=(), writes=()):
        writes = list(writes) + [k for k in reads if isinstance(k, tuple) and k[0] == 'ps' and k not in writes]
        idx = len(s.ops[eng])
        d = s._deps(eng, reads, writes)
        s.ops[eng].append(dict(fn=fn, deps=d, sig=False, kind='op'))
        tok = ('e', eng, idx)
        for k in reads:
            s.readers.setdefault(k, {})[('e', eng)] = tok
        for k in writes:
            s.lastw[k] = tok
            s.readers[k] = {}
        return tok

    def dma(s, q, chan, fns, reads=(), writes=()):
        d = s._deps(q, reads, writes)
        if s.chan_n.get(chan, 0) > 0:
            d[('d', chan)] = max(d.get(('d', chan), -1), s.chan_n[chan])
        s.chan_n[chan] = s.chan_n.get(chan, 0) + len(fns)
        v = s.chan_n[chan]
        s.ops[q].append(dict(fn=fns, deps=d, kind='dma', chan=chan))
        tok = ('d', chan, v)
        for k in reads:
            s.readers.setdefault(k, {})[('d', chan)] = tok
        for k in writes:
            s.lastw[k] = tok
            s.readers[k] = {}
        return tok

    def fence(s):
        last = {e: len(s.ops[e]) - 1 for e in ENGS}
        chans = dict(s.chan_n)
        for e in ENGS:
            d = {}
            for e2 in ENGS:
                if e2 != e and last[e2] >= 0:
                    d[('e', e2)] = last[e2]
            for c, v in chans.items():
                d[('d', c)] = v
            s.ops[e].append(dict(fn=None, deps=d, sig=False, kind='nop'))

    def finish(s):
        d = {('d', c): v for c, v in s.chan_n.items()}
        for e2 in ENGS:
            if e2 != 'sp' and len(s.ops[e2]) > 0:
                d[('e', e2)] = len(s.ops[e2]) - 1
        s.ops['sp'].append(dict(fn=None, deps=d, sig=False, kind='nop'))

    def emit(s, nc, es):
        for e in ENGS:
            for o in s.ops[e]:
                for key, val in o['deps'].items():
                    if key[0] == 'e':
                        tgt = s.ops[key[1]][val]
                        j = val
                        while s.ops[key[1]][j]['kind'] != 'op':
                            j -= 1
                            if j < 0:
                                break
                        if j >= 0:
                            s.ops[key[1]][j]['sig'] = True
        nsig = {}
        for e in ENGS:
            c = 0
            for o in s.ops[e]:
                if o.get('sig'):
                    c += 1
                    o['signum'] = c
                o['cum'] = c
            nsig[e] = c
        sems = {e: [es.enter_context(nc.semaphore("s_%s_%d" % (e, i))) for i in range(max(1, (nsig[e] + EPOCH - 1) // EPOCH))]
                for e in ENGS}
        dsems = {c: es.enter_context(nc.semaphore("d_%d" % i)) for i, c in enumerate(s.chan_n.keys())}
        block = es.enter_context(nc.Block())

        def run(ename, h):
            waited = {}
            for o in s.ops[ename]:
                for key, val in o['deps'].items():
                    if waited.get(key, -1) >= val:
                        continue
                    waited[key] = val
                    if key[0] == 'e':
                        j = val
                        while j >= 0 and s.ops[key[1]][j]['kind'] != 'op':
                            j -= 1
                        if j < 0:
                            continue
                        sn = s.ops[key[1]][j]['signum']
                        h.wait_ge(sems[key[1]][(sn - 1) // EPOCH], (sn - 1) % EPOCH + 1)
                    else:
                        h.wait_ge(dsems[key[1]], 16 * val)
                if o['kind'] == 'op':
                    ins = o['fn'](h)
                    if o['sig']:
                        sn = o['signum']
                        ins.then_inc(sems[ename][(sn - 1) // EPOCH], 1)
                elif o['kind'] == 'dma':
                    for f in o['fn']:
                        f(h).then_inc(dsems[o['chan']], 16)

        @block.tensor
        def _(h):
            run('pe', h)

        @block.scalar
        def _(h):
            run('act', h)

        @block.vector
        def _(h):
            run('dve', h)

        @block.gpsimd
        def _(h):
            run('pool', h)

        @block.sync
        def _(h):
            run('sp', h)


def make_consts(S, P):
    c = {}
    c['ident'] = np.eye(128, dtype=np.float32)
    t = np.arange(128)
    c['tril'] = (t[:, None] >= t[None, :]).astype(np.float32)
    R = np.zeros((128, 128), np.float32)
    for i in range(64):
        R[2 * i + 1, 2 * i] = -1.0
        R[2 * i, 2 * i + 1] = 1.0
    c['rotR'] = R
    inv = (1.0 / (10000.0 ** np.linspace(0.0, 1.0, 128, dtype=np.float32))).astype(np.float32)
    invf = np.repeat(inv, 2)

    def sincos(pos):
        ang = pos.astype(np.float32)[None, :] * invf[:, None]
        sn = np.sin(ang).astype(np.float32)
        cs = np.cos(ang).astype(np.float32)
        return (cs.reshape(2, 128, -1).transpose(1, 0, 2).copy(),
                sn.reshape(2, 128, -1).transpose(1, 0, 2).copy())
    c['cos_p'], c['sin_p'] = sincos(np.arange(S))
    slot = np.arange(128)
    c['cos_s'], c['sin_s'] = sincos(P + (slot % 32))
    lg = np.log(np.array(RET_G, np.float64))
    tl_p = (np.arange(512) % 128).astype(np.float64)
    tl_s = (slot % 32).astype(np.float64)
    c['cross_p'] = np.broadcast_to(np.exp(lg[:, None] * (tl_p[None, :] + 1.0))[None], (128, 4, 512)).astype(np.float32).copy()
    c['kinv_p'] = np.broadcast_to(np.exp(-lg[:, None] * (tl_p[None, :] + 1.0))[None], (128, 4, 512)).astype(np.float32).copy()
    c['cross_s'] = np.broadcast_to(np.exp(lg[:, None] * (tl_s[None, :] + 1.0))[None], (128, 4, 128)).astype(np.float32).copy()
    c['kinv_s'] = np.broadcast_to(np.exp(-lg[:, None] * (tl_s[None, :] + 1.0))[None], (128, 4, 128)).astype(np.float32).copy()
    real = (slot % 32) < 16
    strm = slot // 32
    c['mask_ret_p'] = (t[:, None] <= t[None, :]).astype(np.float32)
    m = (strm[:, None] == strm[None, :]) & real[:, None] & real[None, :] & (slot[:, None] <= slot[None, :])
    c['mask_blk_s'] = m.astype(np.float32)
    c['mask_hg_p'] = ((t[:, None] // 64 == t[None, :] // 64) & (t[:, None] <= t[None, :])).astype(np.float32)
    rm = np.zeros((128, 4), np.float32)
    for i in range(4):
        rm[(strm == i) & real, i] = 1.0
    c['rowmask_s'] = rm
    rp = np.zeros((128, 2), np.float32)
    rp[:64, 0] = 1.0
    rp[64:, 1] = 1.0
    c['rowmask_hp'] = rp
    return c


CONST_SHAPES = None


class Builder:
    def __init__(s, S, P):
        s.S_len = S
        s.P = P
        s.nc = bass.Bass("TRN2", target_bir_lowering=False)
        s.sch = Sched()
        s.es = ExitStack()
        s.rot_i = 0
        s.slot_i = 0
        s.tmp_i = 0
        s.out_ch = 0
        s.dr = {}

    def din(s, name, shape, dt=F32):
        s.dr[name] = s.nc.dram_tensor(name, list(shape), dt, kind="ExternalInput").ap()
        return s.dr[name]

    def dout(s, name, shape):
        s.dr[name] = s.nc.dram_tensor(name, list(shape), F32, kind="ExternalOutput").ap()
        return s.dr[name]

    def dint(s, name, shape, dt=BF16):
        s.dr[name] = s.nc.dram_tensor(name, list(shape), dt, kind="Internal").ap()
        return s.dr[name]

    def sb(s, name, shape, dt=F32):
        return s.es.enter_context(s.nc.sbuf_tensor('sb_' + name, list(shape), dt))

    def mm(s, out, lhsT, rhs, start, stop, r, w, skip=False):
        s.sch.op('pe', lambda h: h.matmul(out, lhsT, rhs, start=start, stop=stop, skip_group_check=skip), r, w)

    def tr(s, out, in_, ident, r, w):
        s.sch.op('pe', lambda h: h.transpose(out, in_, ident), r, w)

    def act(s, out, in_, func, r, w, bias=None, scale=None):
        kw = {}
        if bias is not None:
            kw['bias'] = bias
        if scale is not None:
            kw['scale'] = scale
        s.sch.op('act', lambda h: h.activation(out, in_, func, **kw), r, w)

    def tt(s, out, a, b, op, r, w, eng='dve'):
        s.sch.op(eng, lambda h: h.tensor_tensor(out, a, b, op), r, w)

    def ts(s, out, a, s1, s2, op0, op1, r, w, eng='dve'):
        if op1 is None:
            s.sch.op(eng, lambda h: h.tensor_scalar(out, a, s1, None, op0), r, w)
        else:
            s.sch.op(eng, lambda h: h.tensor_scalar(out, a, s1, s2, op0, op1), r, w)

    def stt(s, out, a, sc, b, op0, op1, r, w):
        s.sch.op('dve', lambda h: h.scalar_tensor_tensor(out, a, sc, b, op0, op1), r, w)

    def cp(s, out, in_, r, w, eng='dve'):
        s.sch.op(eng, lambda h: h.tensor_copy(out, in_), r, w)

    def recip(s, out, in_, r, w):
        s.sch.op('dve', lambda h: h.reciprocal(out, in_), r, w)

    def memset(s, ap, val, w, eng='dve'):
        s.sch.op(eng, lambda h: h.memset(ap, val), (), w)

    def load(s, out, in_, chan, r, w, q='sp', **kw):
        s.sch.dma(q, chan, [lambda h: h.dma_start(out=out, in_=in_, **kw)], r, w)

    def store(s, out, in_, r, w):
        ch = ('st', s.out_ch % 8)
        s.out_ch += 1
        s.sch.dma('pool', ch, [lambda h: h.dma_start(out=out, in_=in_)], r, w)

    def rot(s):
        i = s.rot_i % 4
        s.rot_i += 1
        return i

    def convert_weight(s, name, src2d, R, N, KC, BW, ffn_pair=False, layers=None):
        L = R // (KC * 128)
        BWt = 2 * BW if ffn_pair else BW
        NBLK = (N // 2 // BW) if ffn_pair else (N // BW)
        if name not in s.wb:
            s.wb[name] = s.dint('wb_' + name, [L, NBLK, 128, KC * BWt])
            s.wmeta[name] = (KC, BW, ffn_pair)
        dst = s.wb[name]
        for li in (range(L) if layers is None else layers):
            for kc in range(KC):
                rb = li * KC + kc
                ch = ('cv', s.cv_i % 8)
                s.cv_i += 1
                rows = src2d[rb * 128:(rb + 1) * 128, :]
                dv = dst[li].rearrange("b p (k n) -> p b k n", k=KC)[:, :, kc, :]
                if ffn_pair:
                    fns = []
                    for half in range(2):
                        sv = rows[:, half * (N // 2):(half + 1) * (N // 2)].rearrange("p (b n) -> p b n", n=BW)
                        fns.append(lambda h, sv=sv, dv=dv, half=half: h.dma_start(out=dv[:, :, half * BW:(half + 1) * BW], in_=sv, max_dma_last_dim=4096))
                else:
                    sv = rows.rearrange("p (b n) -> p b n", n=BW)
                    fns = [lambda h, sv=sv, dv=dv: h.dma_start(out=dv, in_=sv, max_dma_last_dim=4096)]
                s.sch.dma('pool', ch, fns, (), [('wbf', name, rb)])

    def wload(s, name, r0, KCn, c0, ncols):
        KC, BW, pair = s.wmeta[name]
        assert KCn == KC and (pair or ncols == BW) and c0 % BW == 0, (name, KCn, c0, ncols)
        BWt = 2 * BW if pair else BW
        i = s.slot_i % NSLOT
        s.slot_i += 1
        assert KC * BWt <= SLOT_EL
        li = r0 // (KC * 128)
        flat = s.wslot[i][:, 0:KC * BWt]
        view = flat.rearrange("p (k n) -> p k n", k=KC)
        key = ('wslot', i)
        s.load(flat, s.wb[name][li, c0 // BW], ('w', i), [('wbf', name, li * KC + k) for k in range(KC)], [key])
        return view, key

    def linear_fm(s, wname, row0, KCn, col0, ncols, rhs_fn, T, evac, blk=512):
        done = 0
        while done < ncols:
            nb = min(blk, ncols - done)
            wv, wkey = s.wload(wname, row0, KCn, col0 + done, nb)
            for m_ in range(nb // 128):
                pi = s.rot()
                ps = s.ps[pi]
                for kc in range(KCn):
                    rap, rkey = rhs_fn(kc)
                    s.mm(ps[:, :T], wv[:, kc, m_ * 128:(m_ + 1) * 128], rap, kc == 0, kc == KCn - 1,
                         [wkey, rkey], [('ps', pi)])
                evac(done // 128 + m_, ps[:, :T], ('ps', pi))
            done += nb

    def linear_tm(s, wname, row0, KCn, col0, ncols, lhs_fn, NB, evac):
        for b in range(ncols // 512):
            wv, wkey = s.wload(wname, row0, KCn, col0 + b * 512, 512)
            for tb in range(NB):
                pi = s.rot()
                ps = s.ps[pi]
                for kc in range(KCn):
                    lap, lkey = lhs_fn(kc, tb)
                    s.mm(ps[:, :512], lap, wv[:, kc, :], kc == 0, kc == KCn - 1, [wkey, lkey], [('ps', pi)])
                evac(tb, b, ps[:, :512], ('ps', pi))

    def rstd_from_sq(s, sq_fn, C, nfeat, T, out_ap, out_key):
        pi = s.rot()
        ps = s.ps[pi]
        for c in range(C):
            ap_, k_ = sq_fn(c)
            s.mm(ps[:, :T], s.ones_bf[:, :], ap_, c == 0, c == C - 1, ['ones', k_], [('ps', pi)])
        s.act(s.rs_tmp[:, :T], ps[:, :T], AF.Sqrt, [('ps', pi), 'eps'], ['tmpf'], bias=s.eps_t[:, 0:1], scale=1.0 / nfeat)
        s.recip(out_ap, s.rs_tmp[:, :T], ['tmpf'], [out_key])

    def keep_warm(s, n):
        for _ in range(n):
            s.mm(s.ps[7][:, :], s.ones_bf[:, :], s.fin_sq[:, :], True, True, ['ones', 'fin_sq'], [('ps', 7)])

    def modulate(s, l, which, T, streams):
        xT, hT, sqT = s.xT, s.hT, s.sqT
        s.act(sqT[:, :, :T], xT[:, :, :T], AF.Square, ['xT'], ['sqT'])
        s.rstd_from_sq(lambda c: (sqT[:, c, :T], 'sqT'), 8, D, T, s.rstd[:, :T], 'rstd')
        if T == 512:
            s.keep_warm(36)
        for c in range(8):
            tmp, tk = (s.tmpf, 'tmpf') if c % 2 == 0 else (s.stage[1], ('stage', 1))
            s.tt(tmp[:, :T], xT[:, c, :T], s.rstd[:, :T], ALU.mult, ['xT', 'rstd'], [tk])
            for (c0, c1, si) in streams:
                s.act(hT[:, c, c0:c1], tmp[:, c0:c1], AF.Identity, [tk, 'comb'], [('hT', c)],
                      bias=s.comb[:, l, 3 * which + 0, c, si:si + 1], scale=s.comb[:, l, 3 * which + 1, c, si:si + 1])

    def evac_out(s, m, ps, pkey, T):
        s.cp(s.outT[:, m, :T], ps, [pkey], [('outT', m)])
        s.act(s.sqT[:, m, :T], ps, AF.Square, [pkey], ['sqT'])

    def postnorm(s, l, which, T, streams):
        s.rstd_from_sq(lambda c: (s.sqT[:, c, :T], 'sqT'), 8, D, T, s.rstd[:, :T], 'rstd')
        if T == 512:
            s.keep_warm(36)
        for c in range(8):
            s.tt(s.tmpf[:, :T], s.outT[:, c, :T], s.rstd[:, :T], ALU.mult, [('outT', c), 'rstd'], ['tmpf'])
            for (c0, c1, si) in streams:
                s.stt(s.xT[:, c, c0:c1], s.tmpf[:, c0:c1], s.comb[:, l, 3 * which + 2, c, si:si + 1], s.xT[:, c, c0:c1],
                      ALU.mult, ALU.add, ['tmpf', 'comb', 'xT'], ['xT'])

    def hkeys(s):
        return [('hT', c) for c in range(8)]

    def ffn(s, l, T, streams):
        s.modulate(l, 1, T, streams)
        aT = s.arena_bf[:, 0:22 * 512].rearrange("p (c t) -> p c t", c=22)
        sg = s.arena_bf[:, 22 * 512:24 * 512].rearrange("p (c t) -> p c t", c=2)
        for jb in range(11):
            wgu, kg = s.wload('ffn_in', l * D, 8, jb * 256, 256)
            ku = kg
            wg = wgu[:, :, 0:256]
            wu = wgu[:, :, 256:512]
            for jj in range(2):
                j = 2 * jb + jj
                pg = s.rot()
                for kc in range(8):
                    s.mm(s.ps[pg][:, :T], wg[:, kc, jj * 128:(jj + 1) * 128], s.hT[:, kc, :T], kc == 0, kc == 7,
                         [kg, ('hT', kc)], [('ps', pg)])
                pu = s.rot()
                for kc in range(8):
                    s.mm(s.ps[pu][:, :T], wu[:, kc, jj * 128:(jj + 1) * 128], s.hT[:, kc, :T], kc == 0, kc == 7,
                         [ku, ('hT', kc)], [('ps', pu)])
                s.act(sg[:, j % 2, :T], s.ps[pg][:, :T], AF.Silu, [('ps', pg)], [('sg', j % 2)])
                s.tt(aT[:, j, :T], sg[:, j % 2, :T], s.ps[pu][:, :T], ALU.mult, [('sg', j % 2), ('ps', pu)], [('aT', j)])
        s.linear_fm('ffn_out', l * FH, 22, 0, D, lambda kc: (aT[:, kc, :T], ('aT', kc)), T,
                    lambda m, ps, pk: s.evac_out(m, ps, pk, T), blk=128)
        s.postnorm(l, 1, T, streams)
        s.sch.fence()

    def gmlp(s, T, streams, sample):
        NB = T // 128
        s.modulate(0, 0, T, streams)
        uT = s.arena_bf[:, 0:24 * T].rearrange("p (c t) -> p c t", c=24)
        vtm = s.arena_bf[:, 24 * T:24 * T + NB * GH].rearrange("p (b f) -> p b f", b=NB)
        s.linear_fm('gmlp_in', 0, 8, 0, GH, lambda kc: (s.hT[:, kc, :T], ('hT', kc)), T,
                    lambda m, ps, pk: s.act(uT[:, m, :T], ps, AF.Gelu, [pk], [('uT', m)]))
        s.linear_tm('gmlp_in', 0, 8, GH, GH, lambda kc, tb: (s.hT[:, kc, tb * 128:(tb + 1) * 128], ('hT', kc)), NB,
                    lambda tb, b, ps, pk: s.act(vtm[:, tb, b * 512:(b + 1) * 512], ps, AF.Gelu, [pk], [('vtm', tb)]))
        for tb in range(NB):
            for b in range(6):
                s.sch.op('dve', lambda h, tb=tb, b=b: h.bn_stats(s.bnst[:, b, :], vtm[:, tb, b * 512:(b + 1) * 512]),
                         [('vtm', tb)], ['bnst'])
            s.sch.op('dve', lambda h: h.bn_aggr(s.bnag[:, :], s.bnst[:, :, :].rearrange("p a b -> p (a b)")), ['bnst'], ['bnag'])
            s.act(s.col_a[:, 0:1], s.bnag[:, 1:2], AF.Sqrt, ['bnag', 'eps'], ['col_a'], bias=s.eps_t[:, 0:1], scale=1.0)
            s.recip(s.col_b[:, 0:1], s.col_a[:, 0:1], ['col_a'], ['col_b'])
            s.stt(s.col_c[:, 0:1], s.bnag[:, 0:1], -1.0, s.col_b[:, 0:1], ALU.mult, ALU.mult, ['bnag', 'col_b'], ['col_c'])
            s.act(vtm[:, tb, :], vtm[:, tb, :], AF.Identity, [('vtm', tb), 'col_b', 'col_c'], [('vtm', tb)],
                  bias=s.col_c[:, 0:1], scale=s.col_b[:, 0:1])
            if sample:
                s.tt(s.big_f[:, 0:GH], vtm[:, tb, :], s.lng_b[:, :], ALU.mult, [('vtm', tb), 'lngb'], ['big_f'])
                s.tt(s.big_f[:, 0:GH], s.big_f[:, 0:GH], s.lnb_b[:, :], ALU.add, ['big_f', 'lngb'], ['big_f'])
                for i in range(4):
                    s.store(s.dr['o_gv'][i * 16:(i + 1) * 16, :], s.big_f[i * 32:i * 32 + 16, 0:GH], ['big_f'], [])
        spw = s.spw_s if sample else s.spw_p
        rows2 = s.rows2_s if sample else s.rows2_p
        for c in range(24):
            g = c // 3
            pi = s.rot()
            ps = s.ps[pi]
            for tb in range(NB):
                s.mm(ps[:, tb * 128:(tb + 1) * 128], s.lrow[0:2, c * 128:(c + 1) * 128], rows2[0:2, g, :], tb == 0, False, ['lrow', 'rows2'], [('ps', pi)], skip=True)
                s.mm(ps[:, tb * 128:(tb + 1) * 128], vtm[:, tb, c * 128:(c + 1) * 128], spw[:, g, :], False, True,
                     [('vtm', tb), 'spw'], [('ps', pi)], skip=True)
            s.stt(uT[:, c, :T], ps[:, :T], s.lng_fm[:, c:c + 1], uT[:, c, :T], ALU.mult, ALU.mult, [('ps', pi), 'lng_fm', ('uT', c)],
                  [('uT', c)])
        s.linear_fm('gmlp_out', 0, 24, 0, D, lambda kc: (uT[:, kc, :T], ('uT', kc)), T,
                    lambda m, ps, pk: s.evac_out(m, ps, pk, T), blk=128)
        s.postnorm(0, 0, T, streams)
        s.sch.fence()

    def attn_common_proj(s, T, NB, qT, kT, vbf, k_out, v_out, row_of):
        s.linear_fm('diff_in', 0, 8, 0, D, lambda kc: (s.hT[:, kc, :T], ('hT', kc)), T,
                    lambda m, ps, pk: s.act(qT[:, m, :T], ps, AF.Copy, [pk], [('qT', m)], scale=0.125))
        s.linear_fm('diff_in', 0, 8, D, D, lambda kc: (s.hT[:, kc, :T], ('hT', kc)), T,
                    lambda m, ps, pk: s.cp(kT[:, m, :T], ps, [pk], [('kT', m)]))

        def ev_k(tb, b, ps, pk):
            st = s.stage[s.tmp_i % 2]
            sk = ('stage', s.tmp_i % 2)
            s.tmp_i += 1
            s.cp(st[:, 0:512], ps, [pk], [sk], eng='dve')
            for (r0, n, d0) in row_of(tb):
                s.store(k_out[d0:d0 + n, b * 512:(b + 1) * 512], st[r0:r0 + n, 0:512], [sk], [])
        s.linear_tm('diff_in', 0, 8, D, D, lambda kc, tb: (s.hT[:, kc, tb * 128:(tb + 1) * 128], ('hT', kc)), NB, ev_k)

        def ev_v(tb, b, ps, pk):
            st = s.stage[s.tmp_i % 2]
            sk = ('stage', s.tmp_i % 2)
            s.tmp_i += 1
            s.cp(st[:, 0:512], ps, [pk], [sk], eng='dve')
            s.act(vbf[:, tb, b * 512:(b + 1) * 512], ps, AF.Copy, [pk], [('vbf', tb)])
            for (r0, n, d0) in row_of(tb):
                s.store(v_out[d0:d0 + n, b * 512:(b + 1) * 512], st[r0:r0 + n, 0:512], [sk], [])
        s.linear_tm('diff_in', 0, 8, 2 * D, D, lambda kc, tb: (s.hT[:, kc, tb * 128:(tb + 1) * 128], ('hT', kc)), NB, ev_v)

    def attn_finalize(s, o1, o2, z1, z2, keys, W, out_ap, out_key):
        f = s.fin
        s.recip(f[:, 0, :W], z1, [keys[2]], [('fin', 0)])
        s.recip(f[:, 1, :W], z2, [keys[3]], [('fin', 1)])
        s.tt(f[:, 0, :W], o1, f[:, 0, :W], ALU.mult, [keys[0], ('fin', 0)], [('fin', 0)])
        s.tt(f[:, 1, :W], o2, f[:, 1, :W], ALU.mult, [keys[1], ('fin', 1)], [('fin', 1)])
        s.stt(f[:, 2, :W], f[:, 1, :W], s.neglam[:, 0:1], f[:, 0, :W], ALU.mult, ALU.add, [('fin', 0), ('fin', 1), 'neglam'], [('fin', 2)])
        s.act(s.fin_sq[:, :W], f[:, 2, :W], AF.Square, [('fin', 2)], ['fin_sq'])
        s.rstd_from_sq(lambda c: (s.fin_sq[:, :W], 'fin_sq'), 1, 128, W, f[:, 3, :W], ('fin', 3))
        s.tt(f[:, 2, :W], f[:, 2, :W], f[:, 3, :W], ALU.mult, [('fin', 2), ('fin', 3)], [('fin', 2)])
        s.act(out_ap, f[:, 2, :W], AF.Identity, [('fin', 2), 'subg'], [out_key], scale=s.subg[:, 0:1])

    def attn_prompt(s, st_i):
        T = 512
        t0 = st_i * 512
        streams = [(0, 512, 0)]
        s.modulate(1, 0, T, streams)
        A = s.arena_bf
        qT = A[:, 0:4096].rearrange("p (c t) -> p c t", c=8)
        kT = A[:, 4096:8192].rearrange("p (c t) -> p c t", c=8)
        vbf = A[:, 8192:12288].rearrange("p (b f) -> p b f", b=4)
        atT = A[:, 12288:16384].rearrange("p (c t) -> p c t", c=8)
        E = A[:, 16384:16384 + 2048].rearrange("p (a t) -> p a t", a=4)
        kspan = [A[:, 18432 + i * 2048:18432 + (i + 1) * 2048] for i in range(2)]
        vspan = [A[:, 22528 + i * 2048:22528 + (i + 1) * 2048].rearrange("p (b d) -> p b d", b=16) for i in range(2)]
        s.attn_common_proj(T, 4, qT, kT, vbf, s.dr['o_kp'], s.dr['o_vp'], lambda tb: [(0, 128, t0 + tb * 128)])
        if st_i < s.S_len // 512 - 1:
            s.store(s.dr['kscr'][:, :, t0:t0 + 512].rearrange("h p t -> p h t"), kT[:, :, :], [('kT', m) for m in range(8)], [('kscr', st_i)])
            for hh in range(8):
                s.store(s.dr['vscr'][hh, :, st_i * 4:(st_i + 1) * 4, :], vbf[:, :, hh * 128:(hh + 1) * 128], [('vbf', b) for b in range(4)],
                        [('vscr', st_i, hh)])
        npre = st_i * 4
        SPB = s.SPB
        for hh in range(8):
            spans = list(range(0, npre, SPB))
            nblk_total = npre + 4

            def load_span(sp0):
                n = min(SPB, npre - sp0)
                bi = s.span_i % 2
                s.span_i += 1
                s.load(kspan[bi][:, 0:n * 128], s.dr['kscr'][hh, :, sp0 * 128:(sp0 + n) * 128], ('ks', bi),
                       [('kscr', j) for j in range(sp0 // 4, (sp0 + n + 3) // 4)], [('kspan', bi)])
                s.load(vspan[bi][:, 0:n, :], s.dr['vscr'][hh, :, sp0:sp0 + n, :], ('vs', bi),
                       [('vscr', j, hh) for j in range(sp0 // 4, (sp0 + n + 3) // 4)], [('vspan', bi)])
                return [(kspan[bi][:, j * 128:(j + 1) * 128], ('kspan', bi), vspan[bi][:, j, :], ('vspan', bi), 0, False) for j in range(n)]

            def qk_exp(bi_, blk):
                (kap, kkey, vap, vkey, c0, diag) = blk
                pa = (bi_ % 2) * 2
                W = 512 - c0
                s.mm(s.ps[pa][:, :W], kap[0:64, :], qT[0:64, hh, c0:512], True, True, [kkey, ('qT', hh)], [('ps', pa)])
                s.mm(s.ps[pa + 1][:, :W], kap[64:128, :], qT[64:128, hh, c0:512], True, True, [kkey, ('qT', hh)], [('ps', pa + 1)])
                s.act(E[:, pa, :W], s.ps[pa][:, :W], AF.Exp, [('ps', pa)], [('E', pa)])
                s.act(E[:, pa + 1, :W], s.ps[pa + 1][:, :W], AF.Exp, [('ps', pa + 1)], [('E', pa + 1)])
                if diag:
                    s.memset(E[64:128, pa, 0:64], 0.0, [('E', pa)], eng='pool')
                    s.memset(E[64:128, pa + 1, 0:64], 0.0, [('E', pa + 1)], eng='pool')

            def pv(bi_, blk):
                (kap, kkey, vap, vkey, c0, diag) = blk
                pa = (bi_ % 2) * 2
                W = 512 - c0
                first = (bi_ == 0)
                last = (bi_ == nblk_total - 1)
                s.mm(s.ps[4][:, c0:512], vap, E[:, pa, :W], first, last, [vkey, ('E', pa)], [('ps', 4)], skip=True)
                s.mm(s.ps[6][:, c0:512], s.ones_bf[:, :], E[:, pa, :W], first, last, ['ones', ('E', pa)], [('ps', 6)], skip=True)
                s.mm(s.ps[5][:, c0:512], vap, E[:, pa + 1, :W], first, last, [vkey, ('E', pa + 1)], [('ps', 5)], skip=True)
                s.mm(s.ps[7][:, c0:512], s.ones_bf[:, :], E[:, pa + 1, :W], first, last, ['ones', ('E', pa + 1)], [('ps', 7)], skip=True)
            prev = None
            cnt = 0
            loaded = {}
            if spans:
                loaded[0] = load_span(spans[0])
            for k_ in range(len(spans) + 1):
                blocks = loaded.pop(k_) if k_ < len(spans) else \
                    [(kT[:, hh, j * 128:(j + 1) * 128], ('kT', hh), vbf[:, j, hh * 128:(hh + 1) * 128], ('vbf', j), j * 128, True) for j in range(4)]
                for bj, blk in enumerate(blocks):
                    qk_exp(cnt, blk)
                    if prev is not None:
                        pv(*prev)
                    prev = (cnt, blk)
                    cnt += 1
                    if bj == 0 and k_ + 1 < len(spans):
                        loaded[k_ + 1] = load_span(spans[k_ + 1])
            pv(*prev)
            s.attn_finalize(s.ps[4][:, :], s.ps[5][:, :], s.ps[6][:, :], s.ps[7][:, :],
                            [('ps', 4), ('ps', 5), ('ps', 6), ('ps', 7)], 512, atT[:, hh, :], ('atT', hh))
        s.linear_fm('diff_out', 0, 8, 0, D, lambda kc: (atT[:, kc, :T], ('atT', kc)), T,
                    lambda m, ps, pk: s.evac_out(m, ps, pk, T))
        s.postnorm(1, 0, T, streams)
        s.sch.fence()

    def attn_sample(s, streams):
        T = 128
        P = s.P
        s.modulate(1, 0, T, streams)
        A = s.arena_bf
        qT = A[:, 0:1024].rearrange("p (c t) -> p c t", c=8)
        kT = A[:, 1024:2048].rearrange("p (c t) -> p c t", c=8)
        vbf = A[:, 2048:3072].rearrange("p (b f) -> p b f", b=1)
        atT = A[:, 3072:4096].rearrange("p (c t) -> p c t", c=8)
        E = A[:, 4096:4096 + 512].rearrange("p (a t) -> p a t", a=2)
        ktm = [A[:, 5120 + i * 1024:5120 + (i + 1) * 1024] for i in range(2)]
        vtm = [A[:, 7168 + i * 1024:7168 + (i + 1) * 1024] for i in range(2)]
        ktT = [A[:, 9216 + i * 1024:9216 + (i + 1) * 1024].rearrange("p (c t) -> p c t", c=8) for i in range(2)]
        s.memset(atT[:, :, :], 0.0, [('atT', c) for c in range(8)])
        s.attn_common_proj(T, 1, qT, kT, vbf, s.dr['o_ks'], s.dr['o_vs'], lambda tb: [(i * 32, 16, i * 16) for i in range(4)])
        nkb = P // 128
        for i in range(4):
            for kb in range(nkb + 1):
                bi = kb % 2
                if kb < nkb:
                    s.load(ktm[bi][:, :], s.dr['ck'][i, kb * 128:(kb + 1) * 128, :], ('ck', bi), [], [('ktm', bi)], q='pool')
                    s.load(vtm[bi][:, :], s.dr['cv'][i, kb * 128:(kb + 1) * 128, :], ('cvv', bi), [], [('vtm', bi)], q='pool')
                    pt = 6 + bi
                    ptb = s.ps[pt][:, :].bitcast(BF16)
                    for c in range(8):
                        s.tr(ptb[:, c * 128:(c + 1) * 128], ktm[bi][:, c * 128:(c + 1) * 128], s.ident_bf[:, :], [('ktm', bi), 'ident'], [('ps', pt)])
                    s.cp(ktT[bi][:, :, :], ptb[:, 0:1024].rearrange("p (c t) -> p c t", c=8), [('ps', pt)], [('ktT', bi)])
                    kTb = ktT[bi]
                    kkeys = [('ktT', bi)]
                    vap = vtm[bi]
                    vkeys = [('vtm', bi)]
                else:
                    kTb = kT
                    kkeys = [('kT', m) for m in range(8)]
                    vap = vbf[:, 0, :]
                    vkeys = [('vbf', 0)]
                for j in range(2):
                    pS = 2 * j + bi
                    for hh in range(8):
                        s.mm(s.ps[pS][:, hh * 16:(hh + 1) * 16], kTb[j * 64:(j + 1) * 64, hh, :], qT[j * 64:(j + 1) * 64, hh, i * 32:i * 32 + 16],
                             True, True, kkeys + [('qT', hh)], [('ps', pS)])
                for j in range(2):
                    s.act(E[:, bi, j * 128:(j + 1) * 128], s.ps[2 * j + bi][:, 0:128], AF.Exp, [('ps', 2 * j + bi)], [('E', bi)])
                if kb == nkb:
                    s.ts(E[:, bi, :], E[:, bi, :], s.rowmask_s[:, i:i + 1], None, ALU.mult, None, [('E', bi), 'rowmask_s'], [('E', bi)])
                for j in range(2):
                    for hh in range(8):
                        reg = slice(j * 128 + hh * 16, j * 128 + (hh + 1) * 16)
                        s.mm(s.ps[4][:, reg], vap[:, hh * 128:(hh + 1) * 128], E[:, bi, reg],
                             (kb == 0 and j == 0 and hh == 0), (kb == nkb), vkeys + [('E', bi)], [('ps', 4)], skip=True)
                s.mm(s.ps[5][:, 0:256], s.ones_bf[:, :], E[:, bi, :], kb == 0, kb == nkb, ['ones', ('E', bi)], [('ps', 5)])
            f = s.fin
            s.recip(f[:, 0, 0:256], s.ps[5][:, 0:256], [('ps', 5)], [('fin', 0)])
            s.tt(f[:, 0, 0:256], s.ps[4][:, 0:256], f[:, 0, 0:256], ALU.mult, [('ps', 4), ('fin', 0)], [('fin', 0)])
            s.stt(f[:, 2, 0:128], f[:, 0, 128:256], s.neglam[:, 0:1], f[:, 0, 0:128], ALU.mult, ALU.add, [('fin', 0), 'neglam'], [('fin', 2)])
            s.act(s.fin_sq[:, 0:128], f[:, 2, 0:128], AF.Square, [('fin', 2)], ['fin_sq'])
            s.rstd_from_sq(lambda c: (s.fin_sq[:, 0:128], 'fin_sq'), 1, 128, 128, f[:, 3, 0:128], ('fin', 3))
            s.tt(f[:, 2, 0:128], f[:, 2, 0:128], f[:, 3, 0:128], ALU.mult, [('fin', 2), ('fin', 3)], [('fin', 2)])
            s.act(atT[:, :, i * 32:i * 32 + 16], f[:, 2, 0:128].rearrange("p (h q) -> p h q", h=8), AF.Identity, [('fin', 2), 'subg'],
                  [('atT', c) for c in range(8)], scale=s.subg[:, 0:1])
        s.linear_fm('diff_out', 0, 8, 0, D, lambda kc: (atT[:, kc, :T], ('atT', kc)), T,
                    lambda m, ps, pk: s.evac_out(m, ps, pk, T))
        s.postnorm(1, 0, T, streams)
        s.sch.fence()

    def retention(s, T, streams, sample, st_i):
        NB = T // 128
        s.modulate(2, 0, T, streams)
        A = s.arena_bf
        off = [0]

        def take(n):
            o = off[0]
            off[0] += n
            return A[:, o:o + n]
        nvar = 4 if sample else 1
        qcT = take(8 * T).rearrange("p (c t) -> p c t", c=8)
        kpT = take(8 * T).rearrange("p (c t) -> p c t", c=8)
        vtm = take(NB * 2048).rearrange("p (b f) -> p b f", b=NB)
        onT = take(16 * T).rearrange("p (c t) -> p c t", c=16)
        kdec = take(nvar * 1024).rearrange("p (v f) -> p v f", v=nvar)
        AT = take(512).rearrange("p (h t) -> p h t", h=4)
        osq = take(512).rearrange("p (c t) -> p c t", c=4)
        qb = take(512)
        sgb = qb
        assert off[0] <= s.ABF
        Fa = s.arena_f
        cosT = Fa[:, 0:1024].rearrange("p (c t) -> p c t", c=2)
        sinT = Fa[:, 1024:2048].rearrange("p (c t) -> p c t", c=2)
        crs = Fa[:, 2048:2560].rearrange("p (h t) -> p h t", h=4)
        kiv = Fa[:, 2560:3072].rearrange("p (h t) -> p h t", h=4)
        t1 = Fa[:, 3072:3584]
        t2 = Fa[:, 3584:4096]
        ohd = Fa[:, 4096:4608].rearrange("p (c t) -> p c t", c=4)
        rs4 = Fa[:, 4608:5120].rearrange("p (h t) -> p h t", h=4)
        if sample:
            s.load(cosT[:, :, :T], s.dr['cos_s'][:, :, :], ('tb', 0), [], ['cosT'])
            s.load(sinT[:, :, :T], s.dr['sin_s'][:, :, :], ('tb', 1), [], ['sinT'])
            s.load(crs[:, :, :], s.dr['cross_s'][:, :, :], ('tb', 2), [], ['crs'])
            s.load(kiv[:, :, :], s.dr['kinv_s'][:, :, :], ('tb', 3), [], ['kiv'])
            mask = s.mask_blk_s
        else:
            t0 = st_i * 512
            s.load(cosT[:, :, :T], s.dr['cos_p'][:, :, t0:t0 + T], ('tb', 0), [], ['cosT'])
            s.load(sinT[:, :, :T], s.dr['sin_p'][:, :, t0:t0 + T], ('tb', 1), [], ['sinT'])
            s.load(crs[:, :, :], s.dr['cross_p'][:, :, 0:128], ('tb', 2), [], ['crs'])
            s.load(kiv[:, :, :], s.dr['kinv_p'][:, :, 0:128], ('tb', 3), [], ['kiv'])
            mask = s.mask_ret_p

        def ev_rot(dst, dkey, scale_tab, sc):
            def ev(m, ps, pk):
                hh = m // 2
                s.act(qb[:, :T], ps, AF.Copy, [pk], ['qb'])
                pr = s.rot()
                s.mm(s.ps[pr][:, :T], s.rotR_bf[:, :], qb[:, :T], True, True, ['rotR', 'qb'], [('ps', pr)])
                s.stt(t1[:, :T], ps, sc, cosT[:, m % 2, :T], ALU.mult, ALU.mult, [pk, 'cosT'], ['t1'])
                s.stt(t2[:, :T], s.ps[pr][:, :T], sc, sinT[:, m % 2, :T], ALU.mult, ALU.mult, [('ps', pr), 'sinT'], ['t2'])
                s.tt(t1[:, :T], t1[:, :T], t2[:, :T], ALU.add, ['t1', 't2'], ['t1'])
                for b in range(NB):
                    s.tt(dst[:, m, b * 128:(b + 1) * 128], t1[:, b * 128:(b + 1) * 128], scale_tab[:, hh, :], ALU.mult, ['t1', 'crs', 'kiv'], [(dkey, m)])
            return ev
        s.linear_fm('ret_in', 0, 8, 0, D, lambda kc: (s.hT[:, kc, :T], ('hT', kc)), T, ev_rot(qcT, 'qcT', crs, 1.0))
        s.linear_fm('ret_in', 0, 8, D, D, lambda kc: (s.hT[:, kc, :T], ('hT', kc)), T, ev_rot(kpT, 'kpT', kiv, 1.0 / 16.0))
        s.linear_tm('ret_in', 0, 8, 2 * D, 2 * D, lambda kc, tb: (s.hT[:, kc, tb * 128:(tb + 1) * 128], ('hT', kc)), NB,
                    lambda tb, b, ps, pk: s.act(vtm[:, tb, b * 512:(b + 1) * 512], ps, AF.Copy, [pk], [('vtm', tb)]))
        L = 16 if sample else 128
        rsf, rsb = s.rstate_f, s.rstate_bf
        for b in range(NB):
            cols = slice(b * 128, (b + 1) * 128)
            pt = s.rot()
            ptb = s.ps[pt][:, :].bitcast(BF16)
            for c in range(8):
                s.tr(ptb[:, c * 128:(c + 1) * 128], kpT[:, c, cols], s.ident_bf[:, :], [('kpT', c), 'ident'], [('ps', pt)])
            for v in range(nvar):
                for hh in range(4):
                    gl = RET_G[hh] ** L
                    if sample:
                        s.ts(kdec[:, v, hh * 256:(hh + 1) * 256], ptb[:, hh * 256:(hh + 1) * 256], s.rowmask_s[:, v:v + 1], gl, ALU.mult, ALU.mult,
                             [('ps', pt), 'rowmask_s'], [('kdec', v)])
                    else:
                        s.act(kdec[:, v, hh * 256:(hh + 1) * 256], ptb[:, hh * 256:(hh + 1) * 256], AF.Copy, [('ps', pt)], [('kdec', v)], scale=gl)
            pS = s.rot()
            for hh in range(4):
                for c in range(2):
                    s.mm(s.ps[pS][:, hh * 128:(hh + 1) * 128], kpT[:, 2 * hh + c, cols], qcT[:, 2 * hh + c, cols], c == 0, c == 1,
                         [('kpT', 2 * hh + c), ('qcT', 2 * hh + c)], [('ps', pS)])
            for hh in range(4):
                s.tt(AT[:, hh, :], s.ps[pS][:, hh * 128:(hh + 1) * 128], mask[:, :], ALU.mult, [('ps', pS), 'masks'], [('AT', hh)])
            for hh in range(4):
                po = 4 + hh
                for dvc in range(4):
                    s.mm(s.ps[po][:, dvc * 128:(dvc + 1) * 128], vtm[:, b, hh * 512 + dvc * 128:hh * 512 + (dvc + 1) * 128], AT[:, hh, :],
                         dvc == 0, False, [('vtm', b), ('AT', hh)], [('ps', po)], skip=True)
            for v in range(nvar):
                if sample:
                    for hh in range(4):
                        s.load(rsf[:, hh, :, :], s.dr['sret'][v, hh, :, :].rearrange("(c p) v -> p c v", p=128), ('rst', hh), [], [('rsf', hh)])
                        s.act(rsb[:, hh, :, :], rsf[:, hh, :, :], AF.Copy, [('rsf', hh)], [('rsb', hh)])
                for hh in range(4):
                    po = 4 + hh
                    for dvc in range(4):
                        if sample:
                            ccols = slice(v * 32, v * 32 + 32)
                            osub = s.ps[po][:, dvc * 128 + v * 32:dvc * 128 + v * 32 + 32]
                        else:
                            ccols = cols
                            osub = s.ps[po][:, dvc * 128:(dvc + 1) * 128]
                        for c in range(2):
                            s.mm(osub, rsb[:, hh, c, dvc * 128:(dvc + 1) * 128], qcT[:, 2 * hh + c, ccols], False, True,
                                 [('rsb', hh), ('qcT', 2 * hh + c)], [('ps', po)], skip=True)
                for hh in range(4):
                    gl = RET_G[hh] ** L
                    for c in range(2):
                        pu = s.rot()
                        s.mm(s.ps[pu][:, :], kdec[:, v, (2 * hh + c) * 128:(2 * hh + c + 1) * 128], vtm[:, b, hh * 512:(hh + 1) * 512], True, True,
                             [('kdec', v), ('vtm', b)], [('ps', pu)])
                        s.stt(rsf[:, hh, c, :], rsf[:, hh, c, :], gl, s.ps[pu][:, :], ALU.mult, ALU.add,
                              [('rsf', hh), ('ps', pu), ('rsb', hh)], [('rsf', hh)])
                    s.act(rsb[:, hh, :, :], rsf[:, hh, :, :], AF.Copy, [('rsf', hh)], [('rsb', hh)])
                    if sample:
                        s.store(s.dr['o_rs'][v, hh, :, :].rearrange("(c p) v -> p c v", p=128), rsf[:, hh, :, :], [('rsf', hh)], [])
            for hh in range(4):
                po = 4 + hh
                s.cp(ohd[:, :, :], s.ps[po][:, :].rearrange("p (c t) -> p c t", c=4), [('ps', po)], ['ohd'])
                s.act(osq[:, :, :], s.ps[po][:, :].rearrange("p (c t) -> p c t", c=4), AF.Square, [('ps', po)], ['osq'])
                s.rstd_from_sq(lambda c: (osq[:, c, :], 'osq'), 4, 512, 128, rs4[:, hh, :], ('rs4', hh))
                for dvc in range(4):
                    s.tt(onT[:, hh * 4 + dvc, cols], ohd[:, dvc, :], rs4[:, hh, :], ALU.mult, ['ohd', ('rs4', hh)], [('onT', hh * 4 + dvc)])

        def ev_g(m, ps, pk):
            s.act(sgb[:, :T], ps, AF.Silu, [pk], ['qb'])
            s.tt(onT[:, m, :T], onT[:, m, :T], sgb[:, :T], ALU.mult, [('onT', m), 'qb'], [('onT', m)])
        s.linear_fm('ret_in', 0, 8, 4 * D, 2 * D, lambda kc: (s.hT[:, kc, :T], ('hT', kc)), T, ev_g)
        s.linear_fm('ret_out', 0, 16, 0, D, lambda kc: (onT[:, kc, :T], ('onT', kc)), T,
                    lambda m, ps, pk: s.evac_out(m, ps, pk, T), blk=256)
        s.postnorm(2, 0, T, streams)
        s.sch.fence()

    def hgrn(s, T, streams, sample):
        NB = T // 128
        CL = 32 if sample else 64
        NCH = 128 // CL
        RL = 16 if sample else 64
        s.modulate(3, 0, T, streams)
        A = s.arena_bf
        off = [0]

        def take(n):
            o = off[0]
            off[0] += n
            return A[:, o:o + n]
        qd = take(8 * T).rearrange("p (c t) -> p c t", c=8)
        kiT = take(8 * T).rearrange("p (c t) -> p c t", c=8)
        vtm = take(NB * 1024).rearrange("p (b f) -> p b f", b=NB)
        onT = take(8 * T).rearrange("p (c t) -> p c t", c=8)
        kmask = take(NCH * 1024).rearrange("p (v f) -> p v f", v=NCH)
        AT = take(1024).rearrange("p (h t) -> p h t", h=8)
        osq = take(1024).rearrange("p (h t) -> p h t", h=8)
        sq_t = take(512)
        f32v = take(4096).bitcast(F32)
        ohd = f32v[:, 0:1024].rearrange("p (h t) -> p h t", h=8)
        rs8 = f32v[:, 1024:2048].rearrange("p (h t) -> p h t", h=8)
        assert off[0] <= s.ABF
        Fa = s.arena_f
        bT = Fa[:, 0:4096].rearrange("p (c t) -> p c t", c=8)
        t1a = Fa[:, 4096:4608]
        t2a = Fa[:, 4608:5120]
        ebl = Fa[:, 5120:5120 + 64].rearrange("p (c h) -> p c h", c=8)
        tb_ = take(2048).bitcast(F32)
        t1b = tb_[:, 0:512]
        t2b = tb_[:, 512:1024]
        assert off[0] <= s.ABF
        mask = s.mask_blk_s if sample else s.mask_hg_p
        rowm = s.rowmask_s if sample else s.rowmask_hp
        hsf, hsb = s.hstate_f, s.hstate_bf

        def ev_f(m, ps, pk):
            t1, t2, k1, k2 = (t1a, t2a, 't1a', 't2a') if m % 2 == 0 else (t1b, t2b, 't1b', 't2b')
            s.act(t1[:, :T], ps, AF.Sigmoid, [pk], [k1])
            s.ts(t1[:, :T], t1[:, :T], s.oml[:, m:m + 1], s.lb[:, m:m + 1], ALU.mult, ALU.add, [k1, 'lb'], [k1])
            s.act(t2[:, :T], t1[:, :T], AF.Ln, [k1], [k2])
            for ch in range(T // CL):
                s.sch.op('dve', lambda h, ch=ch, t2=t2: h.tensor_tensor_scan(bT[:, m, ch * CL:(ch + 1) * CL], s.ones_f[:, 0:CL], t2[:, ch * CL:(ch + 1) * CL],
                                                                             0.0, ALU.mult, ALU.add), [k2, 'ones_f'], [('bT', m)])
            s.ts(t1[:, :T], t1[:, :T], -1.0, 1.0, ALU.mult, ALU.add, [k1], [k1])
            s.act(t2[:, :T], bT[:, m, :T], AF.Exp, [('bT', m)], [k2], scale=-1.0)
            s.tt(kiT[:, m, :T], t1[:, :T], t2[:, :T], ALU.mult, [k1, k2], [('kiT', m)])
        s.linear_fm('hg_in', 0, 8, D, D, lambda kc: (s.hT[:, kc, :T], ('hT', kc)), T, ev_f)

        def ev_q(m, ps, pk):
            t1, t2, k1, k2 = (t1a, t2a, 't1a', 't2a') if m % 2 == 0 else (t1b, t2b, 't1b', 't2b')
            s.act(t1[:, :T], ps, AF.Silu, [pk], [k1])
            s.act(t2[:, :T], bT[:, m, :T], AF.Exp, [('bT', m)], [k2])
            s.tt(qd[:, m, :T], t1[:, :T], t2[:, :T], ALU.mult, [k1, k2], [('qd', m)])
        s.linear_fm('hg_in', 0, 8, 0, D, lambda kc: (s.hT[:, kc, :T], ('hT', kc)), T, ev_q)
        s.linear_tm('hg_in', 0, 8, 2 * D, D, lambda kc, tb: (s.hT[:, kc, tb * 128:(tb + 1) * 128], ('hT', kc)), NB,
                    lambda tb, b, ps, pk: s.act(vtm[:, tb, b * 512:(b + 1) * 512], ps, AF.Copy, [pk], [('vtm', tb)]))
        for b in range(NB):
            cols = slice(b * 128, (b + 1) * 128)
            pt = s.rot()
            ptb = s.ps[pt][:, :].bitcast(BF16)
            for c in range(8):
                s.tr(ptb[:, c * 128:(c + 1) * 128], kiT[:, c, cols], s.ident_bf[:, :], [('kiT', c), 'ident'], [('ps', pt)])
            for v in range(NCH):
                s.ts(kmask[:, v, :], ptb[:, 0:1024], rowm[:, v:v + 1], None, ALU.mult, None, [('ps', pt), 'rowmask'], [('kmask', v)])
            for v in range(NCH):
                s.act(ebl[:, v, :], bT[:, :, b * 128 + v * CL + RL - 1], AF.Exp, [('bT', m) for m in range(8)], [('ebl', v)])
            pS = [s.rot(), s.rot()]
            for hh in range(8):
                s.mm(s.ps[pS[hh // 4]][:, (hh % 4) * 128:(hh % 4 + 1) * 128], kiT[:, hh, cols], qd[:, hh, cols], True, True,
                     [('kiT', hh), ('qd', hh)], [('ps', pS[hh // 4])])
            for hh in range(8):
                s.tt(AT[:, hh, :], s.ps[pS[hh // 4]][:, (hh % 4) * 128:(hh % 4 + 1) * 128], mask[:, :], ALU.mult, [('ps', pS[hh // 4]), 'masks'], [('AT', hh)])
            for hh in range(8):
                po = 4 + hh // 4
                s.mm(s.ps[po][:, (hh % 4) * 128:(hh % 4 + 1) * 128], vtm[:, b, hh * 128:(hh + 1) * 128], AT[:, hh, :], hh % 4 == 0, False,
                     [('vtm', b), ('AT', hh)], [('ps', po)], skip=True)
            for v in range(NCH):
                if sample:
                    s.load(hsf[:, :, :], s.dr['shg'][v, :, :, :].rearrange("h k v -> k h v"), ('hst', 0), [], ['hsf'])
                    s.act(hsb[:, :, :], hsf[:, :, :], AF.Copy, ['hsf'], ['hsb'])
                for hh in range(8):
                    po = 4 + hh // 4
                    osub = s.ps[po][:, (hh % 4) * 128 + v * CL:(hh % 4) * 128 + (v + 1) * CL]
                    s.mm(osub, hsb[:, hh, :], qd[:, hh, b * 128 + v * CL:b * 128 + (v + 1) * CL], False, True,
                         ['hsb', ('qd', hh)], [('ps', po)], skip=True)
                for half in range(2):
                    pu = s.rot()
                    for h4 in range(4):
                        hh = half * 4 + h4
                        s.mm(s.ps[pu][:, h4 * 128:(h4 + 1) * 128], kmask[:, v, hh * 128:(hh + 1) * 128], vtm[:, b, hh * 128:(hh + 1) * 128], h4 == 0, True,
                             [('kmask', v), ('vtm', b)], [('ps', pu)], skip=True)
                    hs = hsf[:, half * 4:half * 4 + 4, :]
                    s.tt(hs, hs, s.ps[pu][:, :].rearrange("p (h d) -> p h d", h=4), ALU.add, ['hsf', ('ps', pu), 'hsb'], ['hsf'])
                for hh in range(8):
                    s.ts(hsf[:, hh, :], hsf[:, hh, :], ebl[:, v, hh:hh + 1], None, ALU.mult, None, ['hsf', ('ebl', v)], ['hsf'])
                s.act(hsb[:, :, :], hsf[:, :, :], AF.Copy, ['hsf'], ['hsb'])
                if sample:
                    s.store(s.dr['o_hs'][v, :, :, :].rearrange("h k v -> k h v"), hsf[:, :, :], ['hsf'], [])
            for half in range(2):
                po = 4 + half
                s.cp(ohd[:, half * 4:half * 4 + 4, :], s.ps[po][:, :].rearrange("p (h t) -> p h t", h=4), [('ps', po)], ['ohd'])
                s.act(osq[:, half * 4:half * 4 + 4, :], s.ps[po][:, :].rearrange("p (h t) -> p h t", h=4), AF.Square, [('ps', po)], ['osq'])
            for hh in range(8):
                s.rstd_from_sq(lambda c, hh=hh: (osq[:, hh, :], 'osq'), 1, 128, 128, rs8[:, hh, :], ('rs8', hh))
                s.stt(onT[:, hh, cols], ohd[:, hh, :], s.hgn[:, hh:hh + 1], rs8[:, hh, :], ALU.mult, ALU.mult, ['ohd', ('rs8', hh), 'hgn'], [('onT', hh)])

        def ev_g(m, ps, pk):
            s.act(sq_t[:, :T], ps, AF.Silu, [pk], ['sq_t'])
            s.tt(onT[:, m, :T], onT[:, m, :T], sq_t[:, :T], ALU.mult, [('onT', m), 'sq_t'], [('onT', m)])
        s.linear_fm('hg_in', 0, 8, 3 * D, D, lambda kc: (s.hT[:, kc, :T], ('hT', kc)), T, ev_g)
        s.linear_fm('hg_out', 0, 8, 0, D, lambda kc: (onT[:, kc, :T], ('onT', kc)), T,
                    lambda m, ps, pk: s.evac_out(m, ps, pk, T))
        s.postnorm(3, 0, T, streams)
        s.sch.fence()

    def load_x(s, src_rows, T):
        NB = T // 128
        for tb in range(NB):
            st = s.stage[tb % 2]
            sk = ('stage', tb % 2)
            for (ap_, p0, n, tb_) in src_rows:
                if tb_ == tb:
                    s.load(st[p0:p0 + n, :], ap_, ('xl', tb % 2), [], [sk])
            for half in range(2):
                pi = s.rot()
                for c4 in range(4):
                    c = half * 4 + c4
                    s.tr(s.ps[pi][:, c4 * 128:(c4 + 1) * 128], st[:, c * 128:(c + 1) * 128], s.ident_f[:, :], [sk, 'ident'], [('ps', pi)])
                s.cp(s.xT[:, half * 4:half * 4 + 4, tb * 128:(tb + 1) * 128], s.ps[pi][:, :].rearrange("p (c t) -> p c t", c=4), [('ps', pi)], ['xT'])

    def store_y(s, dst_rows, T):
        NB = T // 128
        for tb in range(NB):
            st = s.stage[tb % 2]
            sk = ('stage', tb % 2)
            for half in range(2):
                pi = s.rot()
                for c4 in range(4):
                    c = half * 4 + c4
                    s.tr(s.ps[pi][:, c4 * 128:(c4 + 1) * 128], s.xT[:, c, tb * 128:(tb + 1) * 128], s.ident_f[:, :], ['xT', 'ident'], [('ps', pi)])
                s.cp(st[:, half * 512:(half + 1) * 512], s.ps[pi][:, :], [('ps', pi)], [sk])
            for (ap_, p0, n, tb_) in dst_rows:
                if tb_ == tb:
                    s.store(ap_, st[p0:p0 + n, :], [sk], [])

    def load_rows_fm(s, rows_aps, dst, dkey):
        st = s.stage[0]
        sk = ('stage', 0)
        r = 0
        s.memset(st[:, 0:128], 0.0, [sk])
        for ap_ in rows_aps:
            n = ap_.shape[0]
            s.load(st[r:r + n, 0:128], ap_, ('xl', 0), [], [sk])
            r += n
        assert r <= 128
        pi = s.rot()
        s.tr(s.ps[pi][:, 0:128], st[:, 0:128], s.ident_f[:, :], [sk, 'ident'], [('ps', pi)])
        s.cp(dst, s.ps[pi][:, 0:r], [('ps', pi)], [dkey])

    def build(s):
        nc = s.nc
        S, P = s.S_len, s.P
        NST = S // 512
        s.din('xp', [S, D]); s.din('xs', [128, D]); s.din('ck', [4, P, D]); s.din('cv', [4, P, D])
        s.din('sret', [4, 4, 256, 512]); s.din('shg', [4, 8, 128, 128]); s.din('cvec', [5, D]); s.din('pmask', [128, 1])
        s.din('w_ada', [4 * D, 6 * D]); s.din('b_ada', [4 * 48, 128]); s.din('norm_gains', [128, 128])
        s.din('gmlp_w_in', [D, 2 * GH]); s.din('gmlp_ln_g', [1, GH]); s.din('gmlp_ln_b', [1, GH]); s.din('gmlp_ln_g_fm', [24, 128])
        s.din('gmlp_w_s', [8, 128, 128]); s.din('gmlp_b_s', [8, 128]); s.din('gmlp_w_out', [GH, D])
        s.din('diff_w_in', [D, 3 * D]); s.din('diff_lambda', [1, 256]); s.din('diff_subln', [128, 1]); s.din('diff_w_out', [D, D])
        s.din('ret_w_in', [D, 6 * D]); s.din('ret_w_out', [2 * D, D])
        s.din('hgrn_w_in', [D, 4 * D]); s.din('hgrn_norm', [8, 128]); s.din('hgrn_w_out', [D, D]); s.din('hgrn_lb', [32, 128])
        s.din('ffn_w_in', [4 * D, 2 * FH]); s.din('ffn_w_out', [4 * FH, D])
        cshapes = {k: v.shape for k, v in make_consts(S, P).items()}
        for k, shp in cshapes.items():
            s.din(k, list(shp))
        s.dout('o_yp', [S, D]); s.dout('o_ys', [64, D]); s.dout('o_gv', [64, GH]); s.dout('o_kp', [S, D]); s.dout('o_vp', [S, D])
        s.dout('o_ks', [64, D]); s.dout('o_vs', [64, D]); s.dout('o_rp', [4, 256, 512]); s.dout('o_rs', [4, 4, 256, 512])
        s.dout('o_hp', [8, 128, 128]); s.dout('o_hs', [4, 8, 128, 128])
        s.dint('kscr', [8, 128, S]); s.dint('vscr', [8, 128, S // 128, 128])
        s.xT = s.sb('xT', [128, 8, 512]); s.hT = s.sb('hT', [128, 8, 512], BF16); s.outT = s.sb('outT', [128, 8, 512], BF16)
        s.sqT = s.sb('sqT', [128, 8, 512], BF16)
        s.wslot = [s.sb('wslot%d' % i, [128, SLOT_EL], BF16) for i in range(NSLOT)]
        s.ABF = 27136
        s.arena_bf = s.sb('arena_bf', [128, s.ABF], BF16)
        s.arena_f = s.sb('arena_f', [128, 5248], F32)
        s.big_f = s.arena_f
        s.stage = [s.sb('stage%d' % i, [128, 1024]) for i in range(2)]
        s.rstate_f = s.sb('rsf', [128, 4, 2, 512]); s.rstate_bf = s.sb('rsb', [128, 4, 2, 512], BF16)
        s.hstate_f = s.sb('hsf', [128, 8, 128]); s.hstate_bf = s.sb('hsb', [128, 8, 128], BF16)
        s.rstd = s.sb('rstd', [128, 512]); s.tmpf = s.sb('tmpf', [128, 512]); s.rs_tmp = s.tmpf
        s.fin = s.arena_f[:, 0:2048].rearrange("p (a t) -> p a t", a=4); s.fin_sq = s.sb('fin_sq', [128, 512], BF16)
        s.ident_f = s.sb('ident_f', [128, 128]); s.ident_bf = s.sb('ident_bf', [128, 128], BF16)
        s.ones_bf = s.sb('ones_bf', [128, 128], BF16); s.ones_f = s.sb('ones_f', [128, 64])
        s.rotR_bf = s.sb('rotR_bf', [128, 128], BF16)
        s.eps_t = s.sb('eps_t', [128, 1])
        s.comb = s.sb('comb', [128, 4, 6, 8, 5]); s.modraw = s.arena_f[:, 4096:4096 + 240].rearrange("p (j s) -> p j s", j=48)
        s.bada = s.sb('bada', [128, 192]); s.ng = s.sb('ng', [128, 128]); s.cT = s.sb('cT', [128, 40]); s.cT_bf = s.sb('cT_bf', [128, 8, 5], BF16)
        s.lng_fm = s.sb('lng_fm', [128, 24]); s.hgn = s.sb('hgn', [128, 8]); s.hlbT = s.sb('hlbT', [128, 32])
        s.lb = s.sb('lb', [128, 8]); s.oml = s.sb('oml', [128, 8]); s.subg = s.sb('subg', [128, 1]); s.neglam = s.sb('neglam', [128, 1])
        s.lrow = s.sb('lrow', [2, GH], BF16); s.lrow_f = s.arena_bf[0:2, 0:4 * GH].bitcast(F32).rearrange("p (a f) -> p a f", a=2)
        s.rows2_p = s.sb('rows2_p', [2, 8, 128], BF16); s.rows2_s = s.sb('rows2_s', [2, 8, 128], BF16)
        s.rows2_f = s.arena_f[0:2, 0:1024].rearrange("p (g t) -> p g t", g=8)
        s.spw_p = s.sb('spw_p', [128, 8, 128], BF16); s.spw_s = s.sb('spw_s', [128, 8, 128], BF16)
        s.mask_ret_p = s.sb('mask_ret_p', [128, 128]); s.mask_blk_s = s.sb('mask_blk_s', [128, 128]); s.mask_hg_p = s.sb('mask_hg_p', [128, 128])
        s.tril = s.sb('tril', [128, 128])
        s.rowmask_s = s.sb('rowmask_s', [128, 4]); s.rowmask_hp = s.sb('rowmask_hp', [128, 2])
        s.bnst = s.sb('bnst', [128, 6, 6]); s.bnag = s.sb('bnag', [128, 2])
        s.col_a = s.sb('col_a', [128, 1]); s.col_b = s.sb('col_b', [128, 1]); s.col_c = s.sb('col_c', [128, 1])
        s.lam_t = s.sb('lam_t', [1, 256]); s.lam_s = s.sb('lam_s', [1, 8]); s.pmask = s.sb('pmask_t', [128, 1])
        s.ps = [s.es.enter_context(nc.psum_tensor('ps%d' % i, [128, 512], F32)) for i in range(8)]
        s.wb = {}
        s.wmeta = {}
        s.cv_i = 0
        s.span_i = 0
        import os as _os
        s.SPB = int(_os.environ.get('KSPB', '16'))
        dr = s.dr

        def cv_layer(l):
            s.convert_weight('w_ada', dr['w_ada'], 4 * D, 6 * D, 8, 512, layers=[l])
            if l == 0:
                s.convert_weight('gmlp_in', dr['gmlp_w_in'], D, 2 * GH, 8, 512)
                s.convert_weight('gmlp_out', dr['gmlp_w_out'], GH, D, 24, 128)
            elif l == 1:
                s.convert_weight('diff_in', dr['diff_w_in'], D, 3 * D, 8, 512)
                s.convert_weight('diff_out', dr['diff_w_out'], D, D, 8, 512)
            elif l == 2:
                s.convert_weight('ret_in', dr['ret_w_in'], D, 6 * D, 8, 512)
                s.convert_weight('ret_out', dr['ret_w_out'], 2 * D, D, 16, 256)
            else:
                s.convert_weight('hg_in', dr['hgrn_w_in'], D, 4 * D, 8, 512)
                s.convert_weight('hg_out', dr['hgrn_w_out'], D, D, 8, 512)
            s.convert_weight('ffn_in', dr['ffn_w_in'], 4 * D, 2 * FH, 8, 256, ffn_pair=True, layers=[l])
            s.convert_weight('ffn_out', dr['ffn_w_out'], 4 * FH, D, 22, 128, layers=[l])
        for l in range(4):
            cv_layer(l)

        if getattr(s, 'KS', 99) < 1:
            return
        s.load(s.ident_f[:, :], dr['ident'], ('c', 0), [], ['ident_f'])
        s.cp(s.ident_bf[:, :], s.ident_f[:, :], ['ident_f'], ['ident'])
        s.load(s.stage[1][:, 0:128], dr['rotR'], ('c', 1), [], [('stage', 1)])
        s.cp(s.rotR_bf[:, :], s.stage[1][:, 0:128], [('stage', 1)], ['rotR'])
        s.memset(s.ones_bf[:, :], 1.0, ['ones']); s.memset(s.ones_f[:, :], 1.0, ['ones_f']); s.memset(s.eps_t[:, :], EPS, ['eps'])
        s.load(s.mask_ret_p[:, :], dr['mask_ret_p'], ('c', 2), [], ['masks'])
        s.load(s.mask_blk_s[:, :], dr['mask_blk_s'], ('c', 3), [], ['masks'])
        s.load(s.mask_hg_p[:, :], dr['mask_hg_p'], ('c', 4), [], ['masks'])
        s.load(s.tril[:, :], dr['tril'], ('c', 5), [], ['tril'])
        s.load(s.rowmask_s[:, :], dr['rowmask_s'], ('c', 6), [], ['rowmask_s', 'rowmask'])
        s.load(s.rowmask_hp[:, :], dr['rowmask_hp'], ('c', 7), [], ['rowmask'])
        s.load(s.subg[:, :], dr['diff_subln'], ('c', 0), [], ['subg'])
        s.ts(s.subg[:, :], s.subg[:, :], 1.0 - LAM_INIT, None, ALU.mult, None, ['subg'], ['subg'])
        s.sch.lastw['ident'] = s.sch.lastw['ident']
        s.load_rows_fm([dr['b_ada'][0:128, :]], s.bada[:, 0:128], 'bada')
        s.load_rows_fm([dr['b_ada'][128:192, :]], s.bada[:, 128:192], 'bada')
        s.load_rows_fm([dr['norm_gains']], s.ng[:, :], 'ng')
        s.load_rows_fm([dr['cvec'].rearrange("s (c p) -> (s c) p", p=128)], s.cT[:, :], 'cT')
        s.load_rows_fm([dr['gmlp_ln_g_fm']], s.lng_fm[:, :], 'lng_fm')
        s.load_rows_fm([dr['hgrn_norm']], s.hgn[:, :], 'hgn')
        s.load_rows_fm([dr['hgrn_lb']], s.hlbT[:, :], 'hlbT')
        s.act(s.cT_bf[:, :, :], s.cT[:, :].rearrange("p (s c) -> p c s", s=5), AF.Silu, ['cT'], ['cT_bf'])
        hv = s.hlbT[:, :].rearrange("p (l c) -> p l c", l=4)
        s.act(s.hlbT[:, :], s.hlbT[:, :], AF.Exp, ['hlbT'], ['hlbT'])
        s.tt(s.lb[:, :], hv[:, 0, :], hv[:, 1, :], ALU.add, ['hlbT'], ['lb'])
        s.tt(s.lb[:, :], s.lb[:, :], hv[:, 2, :], ALU.add, ['hlbT', 'lb'], ['lb'])
        s.tt(s.oml[:, :], s.lb[:, :], hv[:, 3, :], ALU.add, ['hlbT', 'lb'], ['oml'])
        s.recip(s.oml[:, :], s.oml[:, :], ['oml'], ['oml'])
        s.tt(s.oml[:, :], s.oml[:, :], hv[:, 0, :], ALU.mult, ['oml', 'hlbT'], ['oml'])
        s.ts(s.lb[:, :], s.oml[:, :], -1.0, 1.0, ALU.mult, ALU.add, ['oml'], ['lb'])
        s.load(s.lam_t[:, :], dr['diff_lambda'], ('c', 1), [], ['lam_t'])
        s.tt(s.lam_t[:, 0:64], s.lam_t[:, 0:64], s.lam_t[:, 64:128], ALU.mult, ['lam_t'], ['lam_t'])
        s.tt(s.lam_t[:, 128:192], s.lam_t[:, 128:192], s.lam_t[:, 192:256], ALU.mult, ['lam_t'], ['lam_t'])
        s.sch.op('dve', lambda h: h.tensor_reduce(s.lam_s[:, 0:1], s.lam_t[:, 0:64], mybir.AxisListType.X, ALU.add), ['lam_t'], ['lam_s'])
        s.sch.op('dve', lambda h: h.tensor_reduce(s.lam_s[:, 1:2], s.lam_t[:, 128:192], mybir.AxisListType.X, ALU.add), ['lam_t', 'lam_s'], ['lam_s'])
        s.act(s.lam_s[:, 2:4], s.lam_s[:, 0:2], AF.Exp, ['lam_s'], ['lam_s'])
        s.tt(s.lam_s[:, 4:5], s.lam_s[:, 3:4], s.lam_s[:, 2:3], ALU.subtract, ['lam_s'], ['lam_s'])
        s.ts(s.lam_s[:, 5:6], s.lam_s[:, 4:5], -LAM_INIT, None, ALU.add, None, ['lam_s'], ['lam_s'])
        orow = s.onesrow_f()
        pi = s.rot()
        s.mm(s.ps[pi][:, 0:2], orow, s.lam_s[0:1, 4:6], True, True, ['lam_s', 'onesrow'], [('ps', pi)])
        s.cp(s.neglam[:, :], s.ps[pi][:, 1:2], [('ps', pi)], ['neglam'])

    def onesrow_f(s):
        if not hasattr(s, '_onesrow'):
            s._onesrow = s.sb('onesrow', [1, 128])
            s.memset(s._onesrow[:, :], 1.0, ['onesrow'])
        return s._onesrow[0:1, :]

    def adaln(s, layers):
        dr = s.dr
        if 0 in layers:
            s.load(s.pmask[:, :], dr['pmask'], ('c', 0), [], ['pmask'])
        for l in layers:
            done = 0
            j = 0
            while done < 6 * D:
                wv, wkey = s.wload('w_ada', l * D, 8, done, 512)
                pi = s.rot()
                for m_ in range(4):
                    for kc in range(8):
                        s.mm(s.ps[pi][:, m_ * 8:m_ * 8 + 5], wv[:, kc, m_ * 128:(m_ + 1) * 128], s.cT_bf[:, kc, :], kc == 0, kc == 7,
                             [wkey, 'cT_bf'], [('ps', pi)])
                for m_ in range(4):
                    s.act(s.modraw[:, j, :], s.ps[pi][:, m_ * 8:m_ * 8 + 5], AF.Identity, [('ps', pi), 'bada'], ['modraw'],
                          bias=s.bada[:, l * 48 + j:l * 48 + j + 1])
                    j += 1
                done += 512
            for which in range(2):
                base = which * 24
                for si in range(5):
                    s.cp(s.comb[:, l, 3 * which + 0, :, si], s.modraw[:, base:base + 8, si], ['modraw'], ['comb'])
                    s.stt(s.comb[:, l, 3 * which + 1, :, si], s.modraw[:, base + 8:base + 16, si], 1.0, s.ng[:, l * 32 + which * 16:l * 32 + which * 16 + 8],
                          ALU.add, ALU.mult, ['modraw', 'ng'], ['comb'])
                    s.tt(s.comb[:, l, 3 * which + 2, :, si], s.modraw[:, base + 16:base + 24, si],
                         s.ng[:, l * 32 + which * 16 + 8:l * 32 + which * 16 + 16], ALU.mult, ['modraw', 'ng'], ['comb'])
                for kind in (0, 1):
                    s.ts(s.comb[:, l, 3 * which + kind, :, 0], s.comb[:, l, 3 * which + kind, :, 0], s.pmask[:, 0:1], None, ALU.mult, None,
                         ['comb', 'pmask'], ['comb'])

    def gmlp_setup(s):
        dr = s.dr
        s.load(s.lrow_f[0:1, 0, :], dr['gmlp_ln_g'], ('c', 2), [], ['lrow_f'])
        s.load(s.lrow_f[1:2, 0, :], dr['gmlp_ln_g'], ('c', 3), [], ['lrow_f'])
        s.memset(s.lrow_f[:, 1, :], 1.0, ['lrow_f'])
        s.load(s.lrow_f[0:1, 1, :], dr['gmlp_ln_b'], ('c', 4), [], ['lrow_f'])
        s.recip(s.lrow_f[:, 0, :], s.lrow_f[:, 0, :], ['lrow_f'], ['lrow_f'])
        s.tt(s.lrow[:, :], s.lrow_f[:, 0, :], s.lrow_f[:, 1, :], ALU.mult, ['lrow_f'], ['lrow'])
        for sample in (False, True):
            spw = s.spw_s if sample else s.spw_p
            rows2 = s.rows2_s if sample else s.rows2_p
            T = 128 if sample else 512
            s.memset(s.rows2_f[:, :, :], 0.0, ['rows2_f'])
            for g in range(8):
                st = s.stage[g % 2]
                sk = ('stage', g % 2)
                if sample:
                    s.memset(st[:, 0:128], 0.0, [sk])
                    for i in range(4):
                        s.load(st[i * 32:i * 32 + 16, i * 32:i * 32 + 16], dr['gmlp_w_s'][g, 0:16, 0:16], ('xl', g % 2), [], [sk])
                        s.load(s.rows2_f[1:2, g, i * 32:i * 32 + 16], dr['gmlp_b_s'][g:g + 1, 0:16], ('c', 5), [], ['rows2_f'])
                else:
                    s.load(st[:, 0:128], dr['gmlp_w_s'][g, :, :], ('xl', g % 2), [], [sk])
                    s.load(s.rows2_f[1:2, g, :], dr['gmlp_b_s'][g:g + 1, :], ('c', 5), [], ['rows2_f'])
                s.tt(st[:, 0:128], st[:, 0:128], s.tril[:, :], ALU.mult, [sk, 'tril'], [sk])
                pi = s.rot()
                s.tr(s.ps[pi][:, 0:128], st[:, 0:128], s.ident_f[:, :], [sk, 'ident'], [('ps', pi)])
                s.cp(spw[:, g, :], s.ps[pi][:, 0:128], [('ps', pi)], ['spw'])
                pj = s.rot()
                s.mm(s.ps[pj][0:1, 0:128], s.ones_bf[:, 0:1], spw[:, g, :], True, True, ['ones', 'spw'], [('ps', pj)])
                s.cp(s.rows2_f[0:1, g, :], s.ps[pj][0:1, 0:128], [('ps', pj)], ['rows2_f'])
            s.cp(rows2[:, :, :], s.rows2_f[:, :, :], ['rows2_f'], ['rows2'])

    def program(s):
        S, P = s.S_len, s.P
        NST = S // 512
        dr = s.dr
        import os
        KS = int(os.environ.get('KSTOP', '99'))
        s.KS = KS
        s.build()
        if KS >= 2:
            s.adaln([0])
        if KS >= 3:
            s.gmlp_setup()
        s.sch.fence()
        if KS < 4:
            s.sch.finish(); s.sch.emit(s.nc, s.es); s.es.close(); return s.nc
        s.memset(s.rstate_f[:, :, :, :], 0.0, [('rsf', h) for h in range(4)])
        s.memset(s.rstate_bf[:, :, :, :], 0.0, [('rsb', h) for h in range(4)])
        s.memset(s.hstate_f[:, :, :], 0.0, ['hsf'])
        s.memset(s.hstate_bf[:, :, :], 0.0, ['hsb'])
        pst = [(0, 512, 0)]
        for st_i in range(NST):
            t0 = st_i * 512
            s.load_x([(dr['xp'][t0 + tb * 128:t0 + (tb + 1) * 128, :], 0, 128, tb) for tb in range(4)], 512)
            if KS >= 5: s.gmlp(512, pst, False)
            if KS >= 6: s.ffn(0, 512, pst)
            if st_i == 0:
                s.adaln([1]); s.sch.fence()
            if KS >= 7: s.attn_prompt(st_i)
            if KS >= 8: s.ffn(1, 512, pst)
            if st_i == 0:
                s.adaln([2]); s.sch.fence()
            if KS >= 9: s.retention(512, pst, False, st_i)
            if KS >= 10: s.ffn(2, 512, pst)
            if st_i == 0:
                s.adaln([3]); s.sch.fence()
            if KS >= 11: s.hgrn(512, pst, False)
            if KS >= 12: s.ffn(3, 512, pst)
            s.store_y([(dr['o_yp'][t0 + tb * 128:t0 + (tb + 1) * 128, :], 0, 128, tb) for tb in range(4)], 512)
            s.sch.fence()
        for hh in range(4):
            s.store(dr['o_rp'][hh, :, :].rearrange("(c p) v -> p c v", p=128), s.rstate_f[:, hh, :, :], [('rsf', hh)], [])
        s.store(dr['o_hp'].rearrange("h k v -> k h v"), s.hstate_f[:, :, :], [('hsf')], [])
        s.sch.fence()
        if KS < 13:
            s.sch.finish(); s.sch.emit(s.nc, s.es); s.es.close(); return s.nc
        sst = [(i * 32, (i + 1) * 32, 1 + i) for i in range(4)]
        s.load_x([(dr['xs'][:, :], 0, 128, 0)], 128)
        af2 = s.arena_bf[:, 8192:8192 + 4 * GH].bitcast(F32)
        s.lng_b = af2[:, 0:GH]
        s.lnb_b = af2[:, GH:2 * GH]
        s.load(s.lng_b, dr['gmlp_ln_g'].partition_broadcast(128), ('c', 6), [], ['lngb'])
        s.load(s.lnb_b, dr['gmlp_ln_b'].partition_broadcast(128), ('c', 7), [], ['lngb'])
        if KS >= 14: s.gmlp(128, sst, True)
        if KS >= 15: s.ffn(0, 128, sst)
        if KS >= 16: s.attn_sample(sst)
        if KS >= 17: s.ffn(1, 128, sst)
        if KS >= 18: s.retention(128, sst, True, 0)
        if KS >= 19: s.ffn(2, 128, sst)
        if KS >= 20: s.hgrn(128, sst, True)
        if KS >= 21: s.ffn(3, 128, sst)
        s.store_y([(dr['o_ys'][i * 16:(i + 1) * 16, :], i * 32, 16, 0) for i in range(4)], 128)
        s.sch.finish()
        s.sch.emit(s.nc, s.es)
        s.es.close()
        return s.nc


_CACHE = {}


def kernel(**inp):
    S = inp['x_prompt'].shape[1]
    P = inp['cache_k_diff'].shape[2]
    key = (S, P)
    if key not in _CACHE:
        _CACHE[key] = Builder(S, P).program()
    nc = _CACHE[key]
    consts = make_consts(S, P)
    f = lambda a: np.ascontiguousarray(np.asarray(a, dtype=np.float32))
    shared = {
        'w_ada': f(inp['w_ada']).reshape(4 * D, 6 * D),
        'b_ada': f(inp['b_ada']).reshape(4 * 48, 128),
        'norm_gains': f(inp['norm_gains']).reshape(128, 128),
        'gmlp_w_in': f(inp['gmlp_w_in'][0]), 'gmlp_ln_g': f(inp['gmlp_ln_g'][0]).reshape(1, GH), 'gmlp_ln_b': f(inp['gmlp_ln_b'][0]).reshape(1, GH),
        'gmlp_ln_g_fm': f(inp['gmlp_ln_g'][0]).reshape(24, 128),
        'gmlp_w_s': f(inp['gmlp_w_s'][0]), 'gmlp_b_s': f(inp['gmlp_b_s'][0]), 'gmlp_w_out': f(inp['gmlp_w_out'][0]),
        'diff_w_in': f(inp['diff_w_in'][0]), 'diff_lambda': f(inp['diff_lambda'][0]).reshape(1, 256),
        'diff_subln': f(inp['diff_subln'][0]).reshape(128, 1), 'diff_w_out': f(inp['diff_w_out'][0]),
        'ret_w_in': f(inp['ret_w_in'][0]), 'ret_w_out': f(inp['ret_w_out'][0]),
        'hgrn_w_in': f(inp['hgrn_w_in'][0]), 'hgrn_norm': f(inp['hgrn_norm'][0]).reshape(8, 128), 'hgrn_w_out': f(inp['hgrn_w_out'][0]),
        'hgrn_lb': f(inp['hgrn_lower_bounds']).reshape(32, 128),
        'ffn_w_in': f(inp['ffn_w_in']).reshape(4 * D, 2 * FH), 'ffn_w_out': f(inp['ffn_w_out']).reshape(4 * FH, D),
    }
    shared.update(consts)
    xp, xs = f(inp['x_prompt']), f(inp['x_sample'])
    ck, cv = inp['cache_k_diff'][0], inp['cache_v_diff'][0]
    in_maps = []
    OWN = [0, 1, 4, 5]
    for c in range(8):
        b = OWN.index(c) if c in OWN else 0
        xs_pad = np.zeros((128, D), np.float32)
        for i in range(4):
            xs_pad[i * 32:i * 32 + 16] = xs[4 * c + i]
        m = dict(shared)
        m['xp'] = xp[b] if c in OWN else np.zeros_like(xp[0])
        m['pmask'] = np.full((128, 1), 1.0 if c in OWN else 0.0, np.float32)
        m['xs'] = xs_pad
        m['ck'] = f(ck[4 * c:4 * c + 4]).reshape(4, P, D)
        m['cv'] = f(cv[4 * c:4 * c + 4]).reshape(4, P, D)
        m['sret'] = f(inp['state_retention'][0, 4 * c:4 * c + 4])
        m['shg'] = f(inp['state_hgrn'][0, 4 * c:4 * c + 4])
        m['cvec'] = np.concatenate([f(inp['c_prompt'])[b:b + 1], f(inp['c_sample'])[4 * c:4 * c + 4]], axis=0)
        in_maps.append(m)
    import os
    ncores = int(os.environ.get('KCORES', '8'))
    res = run_bass_kernel_spmd(nc, in_maps[:ncores], core_ids=list(range(ncores))).results
    res = list(res) + [res[0]] * (8 - ncores)
    B = xp.shape[0]
    yp = np.stack([res[OWN[b]]['o_yp'] for b in range(B)])
    ys = np.concatenate([res[c]['o_ys'].reshape(4, 16, D) for c in range(8)])
    gv = np.concatenate([res[c]['o_gv'].reshape(4, 16, GH) for c in range(8)])[None]
    kp = np.stack([res[OWN[b]]['o_kp'].reshape(S, 16, 64) for b in range(B)])[None]
    vp = np.stack([res[OWN[b]]['o_vp'].reshape(S, 8, 128) for b in range(B)])[None]
    ks = np.concatenate([res[c]['o_ks'].reshape(4, 16, 16, 64) for c in range(8)])[None]
    vs = np.concatenate([res[c]['o_vs'].reshape(4, 16, 8, 128) for c in range(8)])[None]
    rp = np.stack([res[OWN[b]]['o_rp'] for b in range(B)])[None]
    rs = np.concatenate([res[c]['o_rs'] for c in range(8)])[None]
    hp = np.stack([res[OWN[b]]['o_hp'] for b in range(B)])[None]
    hs = np.concatenate([res[c]['o_hs'] for c in range(8)])[None]
    return tuple(np.ascontiguousarray(a, dtype=np.float32) for a in (yp, ys, gv, kp, vp, ks, vs, rp, rs, hp, hs))
```
